# Optimizing a Trainium2 kernel written in Bass

```python
import math
import jax, jax.numpy as jnp
from jax import lax
import numpy as np

D_MODEL = 1024
BATCH = 16
SEQ = 2048
DEPTH = 4

MEM_LEN = 256
RMS_EPS = 1e-6
SB_HEADS = 16
SB_HEAD_DIM = 64
SB_WIDTH = SB_HEADS * SB_HEAD_DIM
SB_BLOCK = 128
SSM_EXPAND = 2
SSM_INNER = SSM_EXPAND * D_MODEL
SSM_HEAD_DIM = 64
SSM_HEADS = SSM_INNER // SSM_HEAD_DIM
SSM_GROUPS = 4
SSM_STATE = 128
SSM_CONV = 4
SSM_CHUNK = 128
SSM_CONV_DIM = SSM_INNER + 2 * SSM_GROUPS * SSM_STATE
XA_HEADS = 4
XA_HEAD_DIM = D_MODEL // XA_HEADS
FFN_HIDDEN = ((8 * D_MODEL + 767) // 768) * 256
IN_SIZES = (SB_WIDTH, SB_WIDTH, SB_WIDTH, SSM_INNER, SSM_CONV_DIM, SSM_HEADS, D_MODEL, D_MODEL)
IN_WIDTH = 3 * SB_WIDTH + SSM_INNER + SSM_CONV_DIM + SSM_HEADS + 2 * D_MODEL

kernel_name = "hybrid_stickbreak_ssd_gated_block"


def _split(a, sizes):
    idx = tuple(int(i) for i in np.cumsum(sizes)[:-1])
    return jnp.split(a, idx, axis=-1)


def rms_norm(x, g):
    xf = x.astype(jnp.float32)
    y = xf * lax.rsqrt(jnp.mean(xf * xf, axis=-1, keepdims=True) + RMS_EPS)
    return (y * g.astype(jnp.float32)).astype(x.dtype)


def stick_breaking_attention(q, k, v):
    bsz, seq = q.shape[0], q.shape[1]
    qf = jnp.swapaxes(q.astype(jnp.float32), 1, 2) * (SB_HEAD_DIM ** -0.5)
    kf = jnp.swapaxes(k.astype(jnp.float32), 1, 2)
    vf = jnp.swapaxes(v.astype(jnp.float32), 1, 2)
    outs = []
    for blk in range(seq // SB_BLOCK):
        t0 = blk * SB_BLOCK
        t1 = t0 + SB_BLOCK
        z = jnp.einsum('bhtd,bhsd->bhts', qf[:, :, t0:t1], kf[:, :, :t1])
        causal = jnp.arange(t1)[None, :] < (t0 + jnp.arange(SB_BLOCK))[:, None]
        log_1mb = jnp.where(causal, jax.nn.log_sigmoid(-z), 0.0)
        tail = lax.cumsum(log_1mb, axis=3, reverse=True)
        w = jnp.where(causal, jnp.exp(z + tail), 0.0)
        outs.append(jnp.einsum('bhts,bhsd->bhtd', w, vf[:, :, :t1]))
    o = jnp.concatenate(outs, axis=2)
    return jnp.swapaxes(o, 1, 2).reshape(bsz, seq, SB_WIDTH)


def ssd_chunked(x, dt, a, bm, cm):
    b, l, h, p = x.shape
    g, n = bm.shape[2], bm.shape[3]
    e = h // g
    c = l // SSM_CHUNK
    xc = (x * dt[..., None]).reshape(b, c, SSM_CHUNK, g, e, p)
    a_dt = (dt * a).reshape(b, c, SSM_CHUNK, g, e).transpose(0, 1, 3, 4, 2)
    a_cs = jnp.cumsum(a_dt, axis=-1)
    bc = bm.reshape(b, c, SSM_CHUNK, g, n)
    cc = cm.reshape(b, c, SSM_CHUNK, g, n)
    tril = jnp.tril(jnp.ones((SSM_CHUNK, SSM_CHUNK), dtype=bool))
    decay = jnp.exp(jnp.where(tril, a_cs[..., :, None] - a_cs[..., None, :], -jnp.inf))
    cb = jnp.einsum('bclgn,bcsgn->bcgls', cc, bc)
    y_diag = jnp.einsum('bcgls,bcgels,bcsgep->bclgep', cb, decay, xc)
    decay_to_end = jnp.exp(a_cs[..., -1:] - a_cs)
    states = jnp.einsum('bclgn,bcgel,bclgep->bcgepn', bc, decay_to_end, xc)
    chunk_decay = jnp.exp(a_cs[..., -1])

    def step(prev, inp):
        st, dec = inp
        return prev * dec[..., None, None] + st, prev

    init = jnp.zeros((b, g, e, p, n), dtype=states.dtype)
    _, prev_states = lax.scan(step, init, (jnp.moveaxis(states, 1, 0), jnp.moveaxis(chunk_decay, 1, 0)))
    prev_states = jnp.moveaxis(prev_states, 0, 1)
    y_off = jnp.einsum('bclgn,bcgepn,bcgel->bclgep', cc, prev_states, jnp.exp(a_cs))
    return (y_diag + y_off).reshape(b, l, h, p)


def ssd_branch(z, xbc, dt_raw, conv_w, conv_b, dt_bias, a_log, d_skip, g_norm):
    bsz, seq = xbc.shape[0], xbc.shape[1]
    xbc = lax.conv_general_dilated(
        xbc, conv_w[:, None, :].astype(xbc.dtype), window_strides=(1,),
        padding=[(SSM_CONV - 1, 0)], dimension_numbers=('NWC', 'WIO', 'NWC'),
        feature_group_count=SSM_CONV_DIM)
    xbc = jax.nn.silu(xbc.astype(jnp.float32) + conv_b.astype(jnp.float32))
    xs, bm, cm = _split(xbc, (SSM_INNER, SSM_GROUPS * SSM_STATE, SSM_GROUPS * SSM_STATE))
    dt = jax.nn.softplus(dt_raw.astype(jnp.float32) + dt_bias.astype(jnp.float32))
    a = -jnp.exp(a_log.astype(jnp.float32))
    xh = xs.reshape(bsz, seq, SSM_HEADS, SSM_HEAD_DIM)
    y = ssd_chunked(xh, dt, a,
                    bm.reshape(bsz, seq, SSM_GROUPS, SSM_STATE),
                    cm.reshape(bsz, seq, SSM_GROUPS, SSM_STATE))
    y = y + d_skip.astype(jnp.float32)[:, None] * xh
    y = y.reshape(bsz, seq, SSM_INNER) * jax.nn.silu(z.astype(jnp.float32))
    yg = y.reshape(bsz, seq, SSM_GROUPS, SSM_INNER // SSM_GROUPS)
    yg = yg * lax.rsqrt(jnp.mean(yg * yg, axis=-1, keepdims=True) + RMS_EPS)
    return yg.reshape(bsz, seq, SSM_INNER) * g_norm.astype(jnp.float32)


def memory_cross_attention(h, mem_n, w_xq, w_xkv, w_xo):
    bsz, seq = h.shape[0], h.shape[1]
    q = (h @ w_xq).reshape(bsz, seq, XA_HEADS, XA_HEAD_DIM)
    k, v = _split(mem_n @ w_xkv, (D_MODEL, D_MODEL))
    k = k.reshape(bsz, MEM_LEN, XA_HEADS, XA_HEAD_DIM)
    v = v.reshape(bsz, MEM_LEN, XA_HEADS, XA_HEAD_DIM)
    s = jnp.einsum('bshd,bmhd->bhsm', q.astype(jnp.float32), k.astype(jnp.float32)) * (XA_HEAD_DIM ** -0.5)
    p = jax.nn.softmax(s, axis=-1)
    o = jnp.einsum('bhsm,bmhd->bshd', p, v.astype(jnp.float32)).reshape(bsz, seq, D_MODEL)
    return o.astype(h.dtype) @ w_xo


def swiglu(h, w_gu, w_down):
    gate, up = _split(h @ w_gu, (FFN_HIDDEN, FFN_HIDDEN))
    return (jax.nn.silu(gate) * up) @ w_down


def setup_inputs(seed: int = 0) -> dict:
    key = jax.random.key(seed)
    ks = jax.random.split(key, 32)
    f32 = jnp.float32

    def dense(k, shape, fan_in):
        return jax.random.normal(k, shape, f32) * (fan_in ** -0.5)

    def gain(k, shape):
        return 1.0 + 0.02 * jax.random.normal(k, shape, f32)

    dt0 = jnp.exp(jax.random.uniform(ks[5], (DEPTH, SSM_HEADS), f32,
                                     minval=math.log(1e-3), maxval=math.log(1e-1)))
    dt_bias = dt0 + jnp.log(-jnp.expm1(-dt0))
    a_log = jnp.log(jax.random.uniform(ks[6], (DEPTH, SSM_HEADS), f32, minval=1.0, maxval=16.0))
    return {
        "x": jax.random.normal(ks[0], (BATCH, SEQ, D_MODEL), f32),
        "mem": jax.random.normal(ks[1], (BATCH, MEM_LEN, D_MODEL), f32),
        "g_pre_mix": gain(ks[2], (DEPTH, D_MODEL)),
        "w_in": dense(ks[3], (DEPTH, D_MODEL, IN_WIDTH), D_MODEL),
        "conv_w": dense(ks[4], (DEPTH, SSM_CONV, SSM_CONV_DIM), SSM_CONV),
        "conv_b": 0.01 * jax.random.normal(ks[7], (DEPTH, SSM_CONV_DIM), f32),
        "dt_bias": dt_bias,
        "a_log": a_log,
        "d_skip": 1.0 + 0.1 * jax.random.normal(ks[8], (DEPTH, SSM_HEADS), f32),
        "g_ssm_norm": gain(ks[9], (DEPTH, SSM_INNER)),
        "w_br_att": dense(ks[10], (DEPTH, SB_WIDTH, D_MODEL), SB_WIDTH),
        "w_br_ssm": dense(ks[11], (DEPTH, SSM_INNER, D_MODEL), SSM_INNER),
        "w_mix_out": dense(ks[12], (DEPTH, D_MODEL, D_MODEL), D_MODEL),
        "g_post_mix": gain(ks[13], (DEPTH, D_MODEL)),
        "g_pre_xa": gain(ks[14], (DEPTH, D_MODEL)),
        "g_mem": gain(ks[15], (DEPTH, D_MODEL)),
        "w_xq": dense(ks[16], (DEPTH, D_MODEL, D_MODEL), D_MODEL),
        "w_xkv": dense(ks[17], (DEPTH, D_MODEL, 2 * D_MODEL), D_MODEL),
        "w_xo": dense(ks[18], (DEPTH, D_MODEL, D_MODEL), D_MODEL),
        "g_post_xa": gain(ks[19], (DEPTH, D_MODEL)),
        "g_pre_ffn": gain(ks[20], (DEPTH, D_MODEL)),
        "w_gu": dense(ks[21], (DEPTH, D_MODEL, 2 * FFN_HIDDEN), D_MODEL),
        "w_down": dense(ks[22], (DEPTH, FFN_HIDDEN, D_MODEL), FFN_HIDDEN),
        "g_post_ffn": gain(ks[23], (DEPTH, D_MODEL)),
    }


def reference(x, mem, g_pre_mix, w_in, conv_w, conv_b, dt_bias, a_log, d_skip, g_ssm_norm,
              w_br_att, w_br_ssm, w_mix_out, g_post_mix, g_pre_xa, g_mem, w_xq, w_xkv, w_xo,
              g_post_xa, g_pre_ffn, w_gu, w_down, g_post_ffn):
    bsz, seq = x.shape[0], x.shape[1]
    for l in range(DEPTH):
        h = rms_norm(x, g_pre_mix[l])
        q, k, v, z, xbc, dt_raw, ga, gs = _split(h @ w_in[l], IN_SIZES)
        o_att = stick_breaking_attention(
            q.reshape(bsz, seq, SB_HEADS, SB_HEAD_DIM),
            k.reshape(bsz, seq, SB_HEADS, SB_HEAD_DIM),
            v.reshape(bsz, seq, SB_HEADS, SB_HEAD_DIM)).astype(x.dtype)
        o_ssm = ssd_branch(z, xbc, dt_raw, conv_w[l], conv_b[l], dt_bias[l], a_log[l],
                           d_skip[l], g_ssm_norm[l]).astype(x.dtype)
        merged = jax.nn.sigmoid(ga) * (o_att @ w_br_att[l]) + jax.nn.sigmoid(gs) * (o_ssm @ w_br_ssm[l])
        x = x + rms_norm(merged @ w_mix_out[l], g_post_mix[l])
        h = rms_norm(x, g_pre_xa[l])
        mem_n = rms_norm(mem, g_mem[l])
        x = x + rms_norm(memory_cross_attention(h, mem_n, w_xq[l], w_xkv[l], w_xo[l]), g_post_xa[l])
        h = rms_norm(x, g_pre_ffn[l])
        x = x + rms_norm(swiglu(h, w_gu[l], w_down[l]), g_post_ffn[l])
    return x
```

```python
import contextlib
import numpy as np
import concourse.bass as bass
import concourse.mybir as mybir
from concourse.bass_utils import run_bass_kernel_spmd

F32 = mybir.dt.float32
BF16 = mybir.dt.bfloat16
ACT = mybir.ActivationFunctionType
ALU = mybir.AluOpType

ENGS = ("pe", "act", "dve", "pool", "sp")
SEG = 20000
EPS = 1e-6


class Op:
    __slots__ = ("eng", "fn", "deps", "dma", "signal", "sigidx", "dmaval")

    def __init__(self, eng, fn):
        self.eng = eng
        self.fn = fn
        self.deps = []
        self.dma = None
        self.dmaval = 0
        self.signal = False
        self.sigidx = -1


class Sched:
    def __init__(self, nc):
        self.nc = nc
        self.ops = {e: [] for e in ENGS}
        self.last_w = {}
        self.readers = {}
        self.dma_cnt = {}
        self.bar = None
        self.bar_done = set()

    @staticmethod
    def key(ap):
        if isinstance(ap, str):
            return ap
        t = getattr(ap, "tensor", None)
        return t.name if t is not None else ap.name

    def barrier(self, engs=("pe", "act", "dve")):
        lasts = [self.ops[e][-1] for e in engs if self.ops[e]]
        self.bar = (engs, lasts)
        self.bar_done = set()

    def add(self, eng, fn, reads=(), writes=(), dma_key=None):
        op = Op(eng, fn)
        deps = []
        rk = [self.key(a) for a in reads if a is not None]
        wk = [self.key(a) for a in writes if a is not None]
        for k in rk:
            w = self.last_w.get(k)
            if w is not None:
                deps.append(w)
            if k.startswith("pb"):
                deps.extend(r for r in self.readers.get(k, ()) if r.eng != eng)
        for k in wk:
            w = self.last_w.get(k)
            if w is not None:
                deps.append(w)
            deps.extend(self.readers.get(k, ()))
        barlist = ()
        if self.bar is not None and eng in self.bar[0] and eng not in self.bar_done:
            barlist = self.bar[1]
            self.bar_done.add(eng)
        seen = set()
        for d in deps:
            if id(d) in seen:
                continue
            seen.add(id(d))
            if d.eng == "pe" and eng == "pe" and d.dma is None:
                continue
            op.deps.append(d)
        for d in barlist:
            if id(d) not in seen:
                seen.add(id(d))
                op.deps.append(d)
        for k in rk:
            self.readers.setdefault(k, []).append(op)
        for k in wk:
            self.last_w[k] = op
            self.readers[k] = []
        if dma_key is not None:
            op.dma = dma_key
            c = self.dma_cnt.get(dma_key, 0) + 1
            self.dma_cnt[dma_key] = c
            op.dmaval = 16 * c
        self.ops[eng].append(op)
        return op

    def emit(self, es):
        nc = self.nc
        for e in ENGS:
            for op in self.ops[e]:
                for d in op.deps:
                    if d.dma is None:
                        d.signal = True
        nsig = {}
        for e in ENGS:
            c = 0
            for op in self.ops[e]:
                if op.signal:
                    op.sigidx = c
                    c += 1
            nsig[e] = c
        sems = {}
        for e in ENGS:
            for s in range((nsig[e] + SEG - 1) // SEG):
                sems[(e, s)] = es.enter_context(nc.semaphore(f"s_{e}_{s}"))
        dsems = {}
        for k in self.dma_cnt:
            dsems[k] = es.enter_context(nc.semaphore("d_" + k))
        block = es.enter_context(nc.Block())
        final_waits = [(dsems[k], 16 * c) for k, c in self.dma_cnt.items()]

        def run(e, engobj):
            waited = {}
            for op in self.ops[e]:
                for d in op.deps:
                    if d.dma is not None:
                        sk = ("d", d.dma)
                        sem, val = dsems[d.dma], d.dmaval
                    else:
                        seg = d.sigidx // SEG
                        sk = (d.eng, seg)
                        sem, val = sems[sk], d.sigidx - seg * SEG + 1
                    if waited.get(sk, 0) >= val:
                        continue
                    waited[sk] = val
                    engobj.wait_ge(sem, val)
                ins = op.fn(engobj)
                if op.dma is not None:
                    ins.then_inc(dsems[op.dma], 16)
                elif op.signal:
                    ins.then_inc(sems[(e, op.sigidx // SEG)], 1)
            if e == "sp":
                for sem, val in final_waits:
                    engobj.wait_ge(sem, val)

        @block.tensor
        def _(t):
            run("pe", t)

        @block.scalar
        def _(t):
            run("act", t)

        @block.vector
        def _(t):
            run("dve", t)

        @block.gpsimd
        def _(t):
            run("pool", t)

        @block.sync
        def _(t):
            run("sp", t)


import os
KSTOP = int(os.environ.get("KSTOP", "9"))
KSUB = int(os.environ.get("KSUB", "99"))
KSS = int(os.environ.get("KSS", "99"))
CONV_ENG = os.environ.get("CONV_ENG", "dve")
D = 1024
NSLOT = 3
SLOT_ELEMS = 4096


def weight_slices():
    sl = {}
    sl["dt"] = (8, 32)
    for g in range(4):
        sl[f"x{g}"] = (8, 512)
        sl[f"bc{g}"] = (8, 256)
        sl[f"z{g}"] = (8, 512)
    for hp in range(8):
        sl[f"qkv{hp}"] = (8, 384)
    for j in range(2):
        for nm in ("ba", "ga", "gs", "bsa", "bsb", "mx", "xk", "xv", "xo"):
            sl[f"{nm}{j}"] = (8, 512)
    for hd in range(4):
        sl[f"xq{hd}"] = (8, 256)
    for fs in range(11):
        sl[f"gu{fs}"] = (8, 512)
    for s in range(5):
        sl[f"dn{s}"] = (4, 1024)
    sl["dn5"] = (2, 1024)
    offs = {}
    o = 0
    for k, (kc, n) in sl.items():
        offs[k] = (o, kc, n)
        o += kc * n
    return offs, o


def pack_layer(inp, l):
    offs, tot = weight_slices()
    out = np.empty((128, tot), np.float32)

    def put(name, W):
        o, kc, n = offs[name]
        assert W.shape == (kc * 128, n), (name, W.shape)
        out[:, o:o + kc * n] = W.reshape(kc, 128, n).transpose(1, 0, 2).reshape(128, kc * n)

    w_in = inp["w_in"][l]
    put("dt", w_in[:, 8192:8224])
    for g in range(4):
        put(f"x{g}", w_in[:, 5120 + g * 512:5120 + (g + 1) * 512])
        put(f"bc{g}", np.concatenate([w_in[:, 7168 + g * 128:7168 + (g + 1) * 128],
                                      w_in[:, 7680 + g * 128:7680 + (g + 1) * 128]], axis=1))
        put(f"z{g}", w_in[:, 3072 + g * 512:3072 + (g + 1) * 512])
    for hp in range(8):
        put(f"qkv{hp}", np.concatenate([w_in[:, hp * 128:(hp + 1) * 128],
                                        w_in[:, 1024 + hp * 128:1024 + (hp + 1) * 128],
                                        w_in[:, 2048 + hp * 128:2048 + (hp + 1) * 128]], axis=1))
    for j in range(2):
        c = slice(j * 512, (j + 1) * 512)
        put(f"ba{j}", inp["w_br_att"][l][:, c])
        put(f"ga{j}", w_in[:, 8224 + j * 512:8224 + (j + 1) * 512])
        put(f"gs{j}", w_in[:, 9248 + j * 512:9248 + (j + 1) * 512])
        put(f"bsa{j}", inp["w_br_ssm"][l][0:1024, c])
        put(f"bsb{j}", inp["w_br_ssm"][l][1024:2048, c])
        put(f"mx{j}", inp["w_mix_out"][l][:, c])
        put(f"xk{j}", inp["w_xkv"][l][:, j * 512:(j + 1) * 512])
        put(f"xv{j}", inp["w_xkv"][l][:, 1024 + j * 512:1024 + (j + 1) * 512])
        put(f"xo{j}", inp["w_xo"][l][:, c])
    for hd in range(4):
        put(f"xq{hd}", inp["w_xq"][l][:, hd * 256:(hd + 1) * 256])
    wgu = inp["w_gu"][l]
    for fs in range(11):
        put(f"gu{fs}", np.concatenate([wgu[:, fs * 256:(fs + 1) * 256],
                                       wgu[:, 2816 + fs * 256:2816 + (fs + 1) * 256]], axis=1))
    wd = inp["w_down"][l]
    for s in range(5):
        put(f"dn{s}", wd[s * 512:(s + 1) * 512])
    put("dn5", wd[2560:2816])
    return out


def make_consts():
    j = np.arange(128)[:, None]
    l = np.arange(128)[None, :]
    c = np.zeros((128, 8, 128), np.float32)
    c[:, 0] = (j == l)
    c[:, 1] = (j <= l)
    c[:, 2] = (j > l)
    c[:, 3] = 1.0
    c[:, 4] = (j < l)
    c[:, 5] = -(j >= l).astype(np.float32)
    c[:, 6] = -1.0
    c[:, 7] = 0.0
    return c


class Builder:
    def __init__(self, nseq, seq, depth, layers=None):
        self.nseq, self.seq, self.depth = nseq, seq, depth
        self.NT = seq // 128
        self.NQ = seq // 512
        self.layers = list(range(depth)) if layers is None else layers
        self.offs, self.wtot = weight_slices()
        nc = bass.Bass("TRN2", target_bir_lowering=False)
        self.nc = nc
        self.S = Sched(nc)
        self.es = contextlib.ExitStack()
        self.uid = 0

    def sb(self, name, shape, dt=F32):
        return self.nc.alloc_sbuf_tensor("s_" + name, shape, dt)

    def A(self, name, shape, dt=F32):
        self.uid += 1
        nb = int(np.prod(shape[1:])) * (4 if dt == F32 else 2)
        nb = (nb + 31) // 32 * 32
        t = self.nc.alloc_sbuf_tensor_at(f"{name}{self.uid}", shape, dt, offset=self.aoff)
        self.aoff += nb
        assert self.aoff <= self.aend, (name, self.aoff, self.aend)
        return t

    def _aps(self, *xs):
        return [x for x in xs if x is not None and not isinstance(x, (int, float))]

    def mm(self, out, lhsT, rhs, start=True, stop=True):
        self.S.add("pe", lambda e: e.matmul(out, lhsT, rhs, start=start, stop=stop), reads=[lhsT, rhs], writes=[out])

    def tr(self, out, in_, ident):
        self.S.add("pe", lambda e: e.transpose(out, in_, ident), reads=[in_, ident], writes=[out])

    def act(self, out, in_, func, bias=None, scale=None, accum=None):
        kw = {}
        if bias is not None:
            kw["bias"] = bias
        if scale is not None:
            kw["scale"] = scale
        if accum is not None:
            kw["accum_out"] = accum
        self.S.add("act", lambda e: e.activation(out=out, in_=in_, func=func, **kw),
                   reads=self._aps(in_, bias, scale), writes=self._aps(out, accum))

    def tt(self, out, in0, in1, op, eng="dve"):
        self.S.add(eng, lambda e: e.tensor_tensor(out=out, in0=in0, in1=in1, op=op), reads=[in0, in1], writes=[out])

    def ts(self, out, in0, s1, s2, op0, op1=None, eng="dve"):
        if op1 is None:
            fn = lambda e: e.tensor_scalar(out=out, in0=in0, scalar1=s1, scalar2=None, op0=op0)
        else:
            fn = lambda e: e.tensor_scalar(out=out, in0=in0, scalar1=s1, scalar2=s2, op0=op0, op1=op1)
        self.S.add(eng, fn, reads=self._aps(in0, s1, s2), writes=[out])

    def stt(self, out, in0, scalar, in1, op0, op1, eng="dve"):
        self.S.add(eng, lambda e: e.scalar_tensor_tensor(out=out, in0=in0, scalar=scalar, in1=in1, op0=op0, op1=op1),
                   reads=self._aps(in0, scalar, in1), writes=[out])

    def cp(self, out, in_, eng="dve"):
        if eng == "act":
            self.S.add("act", lambda e: e.copy(out=out, in_=in_), reads=[in_], writes=[out])
        else:
            self.S.add(eng, lambda e: e.tensor_copy(out=out, in_=in_), reads=[in_], writes=[out])

    def memset(self, ap, val, eng="dve"):
        self.S.add(eng, lambda e: e.memset(ap, val), writes=[ap])

    def recip(self, out, in_):
        self.S.add("dve", lambda e: e.reciprocal(out=out, in_=in_), reads=[in_], writes=[out])

    def dma(self, eng, out, in_, sbuf_side, after_bar=False, store=False):
        k = self.S.key(sbuf_side)
        if not store:
            op = self.S.add(eng, lambda e: e.dma_start(out=out, in_=in_), writes=[out], dma_key=k)
        else:
            op = self.S.add(eng, lambda e: e.dma_start(out=out, in_=in_), reads=[in_], dma_key=k)
        if after_bar and self.S.bar is not None:
            for d in self.S.bar[1]:
                if d not in op.deps:
                    op.deps.append(d)

    def wload(self, l, name):
        o, kc, n = self.offs[name]
        slot = self.wslots[self.wnext % NSLOT]
        self.wnext += 1
        dst = slot[:, 0:kc * n]
        src = self.wpack[l, :, o:o + kc * n]
        self.S.add("pool", lambda e: e.dma_start(out=dst, in_=src), writes=[slot], dma_key=self.S.key(slot))
        return slot[:, 0:kc * n].rearrange("p (k n) -> p k n", k=kc)

    def rstd(self, ss, n):
        rs = self.stat()
        self.act(rs, ss, ACT.Ln, bias=self.epsb[:, 0:1], scale=1.0 / n)
        self.act(rs, rs, ACT.Exp, scale=-0.5)
        return rs

    def stat(self, w=1):
        t = self.stats[self.statn % len(self.stats)]
        self.statn += 1
        return t[:, 0:w]

    def bank(self):
        b = self.banks[self.bankn % 6]
        self.bankn += 1
        return b

    def build(self):
        nc, S = self.nc, self.S
        NT, NQ, nseq, seq = self.NT, self.NQ, self.nseq, self.seq
        DEPTH = self.depth
        x_in = nc.dram_tensor("x", [nseq, seq, D], F32, kind="ExternalInput").ap()
        mem_in = nc.dram_tensor("mem", [nseq, 256, D], F32, kind="ExternalInput").ap()
        self.wpack = nc.dram_tensor("wpack", [DEPTH, 128, self.wtot], F32, kind="ExternalInput").ap()
        cst_in = nc.dram_tensor("cst", [128, 8 * 128], F32, kind="ExternalInput").ap()
        gcols_in = nc.dram_tensor("gcols", [128, DEPTH * 24], F32, kind="ExternalInput").ap()
        convT_in = nc.dram_tensor("convT", [128, DEPTH * 120], F32, kind="ExternalInput").ap()
        hp_in = nc.dram_tensor("hparams", [3, DEPTH * 32], F32, kind="ExternalInput").ap()
        grow_in = nc.dram_tensor("grows", [DEPTH, 5, D], F32, kind="ExternalInput").ap()
        gssm_in = nc.dram_tensor("gssm", [DEPTH, 2048], F32, kind="ExternalInput").ap()
        y_out = nc.dram_tensor("y", [nseq, seq, D], F32, kind="ExternalOutput").ap()

        self.X = [self.sb(f"X{i}", [128, D]) for i in range(NT)]
        self.HT = [self.sb(f"HT{j}", [128, 8, 512], BF16) for j in range(NQ)]
        self.wslots = [self.sb(f"ws{i}", [128, SLOT_ELEMS], BF16) for i in range(NSLOT)]
        self.wnext = 0
        cf = self.sb("cf", [128, 8, 128])
        cb = self.sb("cb", [128, 8, 128], BF16)
        gcols = self.sb("gcols", [128, DEPTH * 24])
        convT = self.sb("convT", [128, DEPTH * 120])
        dtb = self.sb("dtb", [128, DEPTH * 32])
        alog = self.sb("alog", [128, DEPTH * 32])
        aneg = self.sb("aneg", [128, DEPTH * 32])
        dsk = self.sb("dsk", [128, DEPTH * 32])
        self.epsb = self.sb("epsb", [128, 1])
        GB = self.sb("GB", [128, D])
        GS = self.sb("GS", [128, 512])
        Sst = [self.sb(f"Sst{g}", [128, 512]) for g in range(4)]
        halo = self.sb("halo", [128, 24, 3], BF16)
        self.stats = [self.sb(f"st{i}", [128, 2]) for i in range(16)]
        self.statn = 0
        zb = self.sb("zb", [128, 512], BF16)
        pj = self.sb("pj", [128, 512], BF16)
        ptmp = [self.sb(f"ptmp{i}", [128, 512]) for i in range(2)]
        self.banks = [nc.alloc_psum_tensor(f"pb{i}", [128, 512], F32) for i in range(8)]
        self.bankn = 0
        self.aoff0 = (nc._sbuf_addr_for_side("left") + 63) // 64 * 64
        self.aend = nc._sbuf_addr_for_side("right")
        self.aoff = self.aoff0
        ident_b, triLE_f, sGT_f, ones_f, ones_b = cb[:, 0, :], cf[:, 1, :], cf[:, 2, :], cf[:, 3, :], cb[:, 3, :]
        maskLT_b, negTri_b, negOnes_b = cb[:, 4, :], cb[:, 5, :], cb[:, 6, :]

        self.dma("sp", cf[:].rearrange("p a b -> p (a b)"), cst_in[:, :], cf)
        self.dma("sp", gcols[:], gcols_in[:, :], gcols)
        self.dma("sp", convT[:], convT_in[:, :], convT)
        self.dma("sp", dtb[:], hp_in[0:1, :].partition_broadcast(128), dtb)
        self.dma("sp", alog[:], hp_in[1:2, :].partition_broadcast(128), alog)
        self.dma("sp", dsk[:], hp_in[2:3, :].partition_broadcast(128), dsk)
        self.cp(cb[:], cf[:])
        self.memset(self.epsb[:], EPS)
        self.memset(zb[:], 0.0)
        self.act(aneg[:], alog[:], ACT.Exp)
        self.ts(aneg[:], aneg[:], -1.0, None, ALU.mult)

        def prenorm(l, which, tiles):
            goff = l * 24 + which * 8
            mark = self.aoff
            junks = [self.A("junk", [128, D], BF16) for _ in range(2)]
            xns = [self.A("xn", [128, D], BF16) for _ in range(2)]
            for n_, i in enumerate(tiles):
                j, ci = i // 4, i % 4
                junk, xn = junks[n_ % 2], xns[n_ % 2]
                ss = self.stat()
                self.act(junk[:], self.X[i][:], ACT.Square, accum=ss)
                rs = self.rstd(ss, D)
                self.ts(xn[:], self.X[i][:], rs, None, ALU.mult)
                pb = self.bank()
                pbb = pb[:].bitcast(BF16)
                for c in range(8):
                    self.tr(pbb[:, c * 128:(c + 1) * 128], xn[:, c * 128:(c + 1) * 128], ident_b)
                self.tt(self.HT[j][:, :, ci * 128:(ci + 1) * 128],
                        pbb.rearrange("p (c t) -> p c t", c=8),
                        gcols[:, goff:goff + 8].unsqueeze(2).to_broadcast([128, 8, 128]), ALU.mult)
            self.aoff = mark
            S.barrier()

        def postnorm_add(l, grow, i, banks2):
            ss2 = self.stat(2)
            junk = pj
            for b in range(2):
                self.act(junk[:], banks2[b][:], ACT.Square, accum=ss2[:, b:b + 1])
            ss = self.stat()
            self.tt(ss, ss2[:, 0:1], ss2[:, 1:2], ALU.add)
            rs = self.rstd(ss, D)
            for b in range(2):
                tmp = ptmp[b]
                self.stt(tmp[:], banks2[b][:], rs, GB[:, b * 512:(b + 1) * 512], ALU.mult, ALU.mult)
                self.tt(self.X[i][:, b * 512:(b + 1) * 512], self.X[i][:, b * 512:(b + 1) * 512], tmp[:], ALU.add)

        def load_gb(l, which):
            self.dma("sp", GB[:], grow_in[l, which:which + 1, :].partition_broadcast(128), GB)

        def ssd_pass(l, q, OS):
            mark = self.aoff
            wdt = self.wload(l, "dt")
            dt_all = self.A("dt_all", [128, 4, 32])
            adt_all = self.A("adt_all", [128, 4, 32])
            pb = self.bank()
            for ci in range(4):
                for k in range(8):
                    self.mm(pb[:, ci * 32:(ci + 1) * 32], self.HT[q][:, k, ci * 128:(ci + 1) * 128], wdt[:, k, :],
                            start=(k == 0), stop=(k == 7))
            self.tt(dt_all[:], pb[:, 0:128].rearrange("p (c h) -> p c h", c=4),
                    dtb[:, l * 32:(l + 1) * 32].unsqueeze(1).to_broadcast([128, 4, 32]), ALU.add)
            self.act(dt_all[:], dt_all[:], ACT.Exp)
            self.act(dt_all[:], dt_all[:], ACT.Ln, bias=1.0)
            self.tt(adt_all[:], dt_all[:], aneg[:, l * 32:(l + 1) * 32].unsqueeze(1).to_broadcast([128, 4, 32]), ALU.mult)
            acs_all = self.A("acs_all", [128, 256])
            eacs_all = self.A("eacs_all", [128, 4, 32])
            dte_all = self.A("dte_all", [128, 4, 32])
            cd_all = self.A("cd_all", [128, 4, 32])
            pa = self.bank()
            adt_flat = adt_all[:].rearrange("p c h -> p (c h)")
            self.mm(pa[:, 0:128], triLE_f, adt_flat)
            self.mm(pa[:, 128:256], ones_f, adt_flat)
            self.cp(acs_all[:], pa[:, 0:256])
            self.tt(dte_all[:].rearrange("p c h -> p (c h)"), acs_all[:, 128:256], acs_all[:, 0:128], ALU.subtract)
            self.act(eacs_all[:].rearrange("p c h -> p (c h)"), acs_all[:, 0:128], ACT.Exp)
            self.act(dte_all[:].rearrange("p c h -> p (c h)"), dte_all[:].rearrange("p c h -> p (c h)"), ACT.Exp)
            self.act(cd_all[:].rearrange("p c h -> p (c h)"), acs_all[:, 128:256], ACT.Exp)
            mark2 = self.aoff
            if KSUB < 1:
                return
            for g in range(4):
                self.aoff = mark2
                S.barrier()
                self.dma("sp", GS[:], gssm_in[l:l + 1, g * 512:(g + 1) * 512].partition_broadcast(128), GS)
                wx = self.wload(l, f"x{g}")
                wbc = self.wload(l, f"bc{g}")
                xT = self.A("xT", [128, 4, 512], BF16)
                BT = self.A("BT", [128, 512], BF16)
                CT = self.A("CT", [128, 512], BF16)
                Sb = self.A("Sb", [128, 512], BF16)
                if q == 0:
                    self.memset(Sst[g][:], 0.0)
                    self.memset(halo[:, g * 6:(g + 1) * 6, :], 0.0)
                self.cp(Sb[:], Sst[g][:], eng="act")
                raws = [self.A("raw", [128, 516], BF16) for _ in range(3)]
                dgs = [self.A("dg", [128, 4, 128], BF16) for _ in range(6)]

                def chunk_of(fc):
                    return g * 4 + fc if fc < 4 else (16 + g if fc == 4 else 20 + g)

                for fc in range(6):
                    col = l * 120 + chunk_of(fc) * 4
                    self.tt(dgs[fc][:], cf[:, 0, :].unsqueeze(1).to_broadcast([128, 4, 128]),
                            convT[:, col:col + 4].unsqueeze(2).to_broadcast([128, 4, 128]), ALU.mult)
                pbs = {}

                def proj(fc):
                    pb = self.bank()
                    pbs[fc] = pb
                    for k in range(8):
                        if fc < 4:
                            lhsT = wx[:, k, fc * 128:(fc + 1) * 128]
                        else:
                            lhsT = wbc[:, k, (fc - 4) * 128:(fc - 3) * 128]
                        self.mm(pb[:], lhsT, self.HT[q][:, k, :], start=(k == 0), stop=(k == 7))

                def post(fc):
                    raw = raws[fc % 3]
                    pb = pbs[fc]
                    chunk = chunk_of(fc)
                    hl = halo[:, g * 6 + fc, :]
                    self.cp(raw[:, 0:3], hl)
                    self.cp(raw[:, 3:515], pb[:], eng="act")
                    self.cp(hl, raw[:, 512:515])
                    pcv = self.bank()
                    for kk in range(4):
                        self.mm(pcv[:], dgs[fc][:, kk, :], raw[:, kk:kk + 512], start=(kk == 0), stop=(kk == 3))
                    dst = xT[:, fc, :] if fc < 4 else (BT[:] if fc == 4 else CT[:])
                    self.act(dst, pcv[:], ACT.Silu, bias=convT[:, l * 120 + 96 + chunk:l * 120 + 96 + chunk + 1])

                proj(0)
                proj(1)
                for fc in range(6):
                    post(fc)
                    if fc + 2 < 6:
                        proj(fc + 2)
                if KSUB < 2:
                    continue
                wz = self.wload(l, f"z{g}")
                self.aoff -= 3 * 1056 + 6144
                S.barrier()
                zs_all = self.A("zs_all", [128, 4, 512])
                for ci in range(4):
                    pz = self.bank()
                    for k in range(8):
                        self.mm(pz[:], self.HT[q][:, k, ci * 128:(ci + 1) * 128], wz[:, k, :], start=(k == 0), stop=(k == 7))
                    self.act(zs_all[:, ci, :], pz[:], ACT.Silu)
                xdts = [self.A("xdt", [128, 512], BF16) for _ in range(2)]
                Btoks = [self.A("Btok", [128, 128], BF16) for _ in range(2)]
                xtok = self.A("xtok", [128, 512], BF16)
                R = self.A("R", [128, 4, 128])
                Dm = self.A("Dm", [128, 8, 128], BF16)
                cbm = self.A("cbm", [128, 128], BF16)
                xD = self.A("xD", [128, 512], BF16)
                ytmp = self.A("ytmp", [128, 512])
                junk = self.A("junk", [128, 512], BF16)
                ob = self.A("ob", [128, 512], BF16)
                xdte = self.A("xdte", [128, 512], BF16)
                dskg = dsk[:, l * 32 + g * 8:l * 32 + (g + 1) * 8]

                def genA(ci):
                    c0 = ci * 128
                    csl = slice(c0, c0 + 128)
                    dt_g = dt_all[:, ci, g * 8:(g + 1) * 8]
                    adt_g = adt_all[:, ci, g * 8:(g + 1) * 8]
                    xdt, Btok = xdts[ci % 2], Btoks[ci % 2]
                    py = self.banks[6 + ci % 2]
                    pb = self.bank()
                    pbb = pb[:].bitcast(BF16)
                    for fc in range(4):
                        self.tr(pbb[:, fc * 128:(fc + 1) * 128], xT[:, fc, csl], ident_b)
                    self.tr(pbb[:, 512:640], BT[:, csl], ident_b)
                    self.cp(xtok[:], pbb[:, 0:512], eng="act")
                    self.cp(Btok[:], pbb[:, 512:640])
                    yield
                    for hh in range(2):
                        self.tt(R[:], triLE_f.unsqueeze(1).to_broadcast([128, 4, 128]),
                                adt_g[:, hh * 4:(hh + 1) * 4].unsqueeze(2).to_broadcast([128, 4, 128]), ALU.mult)
                        pe_ = self.bank()
                        self.mm(pe_[:], sGT_f, R[:].rearrange("p a b -> p (a b)"))
                        self.act(Dm[:, hh * 4:(hh + 1) * 4, :].rearrange("p a b -> p (a b)"), pe_[:], ACT.Exp)
                        yield
                    pc = self.bank()
                    self.mm(pc[:, 0:128], BT[:, csl], CT[:, csl])
                    self.tt(cbm[:], pc[:, 0:128], triLE_f, ALU.mult)
                    self.tt(Dm[:], Dm[:], cbm[:].unsqueeze(1).to_broadcast([128, 8, 128]), ALU.mult)
                    yield
                    x3 = xtok[:].rearrange("p (h d) -> p h d", h=8)
                    self.tt(xdt[:].rearrange("p (h d) -> p h d", h=8), x3, dt_g.unsqueeze(2).to_broadcast([128, 8, 64]), ALU.mult)
                    self.tt(xD[:].rearrange("p (h d) -> p h d", h=8), x3, dskg.unsqueeze(2).to_broadcast([128, 8, 64]), ALU.mult)
                    yield
                    for h in range(8):
                        self.mm(py[:, h * 64:(h + 1) * 64], Dm[:, h, :], xdt[:, h * 64:(h + 1) * 64], start=(h == 0), stop=False)
                    self.mm(py[:], ident_b, xD[:], start=False, stop=True)
                    yield

                def genB(ci):
                    c0 = ci * 128
                    csl = slice(c0, c0 + 128)
                    xdt, Btok = xdts[ci % 2], Btoks[ci % 2]
                    zs = zs_all[:, ci, :]
                    gsl = slice(g * 8, (g + 1) * 8)
                    py = self.banks[6 + ci % 2]
                    po = self.bank()
                    self.mm(po[:], CT[:, csl], Sb[:])
                    self.tt(ytmp[:].rearrange("p (h d) -> p h d", h=8), po[:].rearrange("p (h d) -> p h d", h=8),
                            eacs_all[:, ci, gsl].unsqueeze(2).to_broadcast([128, 8, 64]), ALU.mult)
                    self.tt(ytmp[:], ytmp[:], py[:], ALU.add)
                    yield
                    self.tt(ytmp[:], ytmp[:], zs, ALU.mult)
                    ss = self.stat()
                    self.act(junk[:], ytmp[:], ACT.Square, accum=ss)
                    rs = self.rstd(ss, 512)
                    self.stt(ob[:], ytmp[:], rs, GS[:], ALU.mult, ALU.mult)
                    yield
                    pt = self.bank()
                    ptb = pt[:].bitcast(BF16)
                    for fc in range(4):
                        self.tr(ptb[:, fc * 128:(fc + 1) * 128], ob[:, fc * 128:(fc + 1) * 128], ident_b)
                    self.cp(OS[:, g * 4:(g + 1) * 4, csl], ptb[:, 0:512].rearrange("p (c t) -> p c t", c=4), eng="act")
                    yield
                    self.tt(xdte[:].rearrange("p (h d) -> p h d", h=8), xdt[:].rearrange("p (h d) -> p h d", h=8),
                            dte_all[:, ci, gsl].unsqueeze(2).to_broadcast([128, 8, 64]), ALU.mult)
                    psn = self.bank()
                    self.mm(psn[:], Btok[:], xdte[:])
                    self.tt(Sst[g][:].rearrange("p (h d) -> p h d", h=8), Sst[g][:].rearrange("p (h d) -> p h d", h=8),
                            cd_all[:, ci, gsl].unsqueeze(2).to_broadcast([128, 8, 64]), ALU.mult)
                    self.tt(Sst[g][:], Sst[g][:], psn[:], ALU.add)
                    self.cp(Sb[:], Sst[g][:], eng="act")
                    yield

                def run_gens(gens):
                    gens = list(gens)
                    while gens:
                        for g_ in list(gens):
                            try:
                                next(g_)
                            except StopIteration:
                                gens.remove(g_)

                run_gens([genA(0)])
                for ci in range(4):
                    run_gens([genB(ci)] + ([genA(ci + 1)] if ci < 3 else []))
            self.aoff = mark
            S.barrier()

        def attn_pass(l, q, OA):
            mark = self.aoff
            nkb = 4 * (q + 1)
            for hp in range(8):
                self.aoff = mark
                S.barrier()
                w = self.wload(l, f"qkv{hp}")
                kT = self.A("kT", [128, (q + 1) * 512], BF16)
                v = self.A("v", [128, nkb, 128], BF16)
                pb = self.bank()
                for k in range(8):
                    self.mm(pb[:], w[:, k, 0:128], self.HT[q][:, k, :], start=(k == 0), stop=(k == 7))
                qTm = [self.A("qTm", [128, 512], BF16) for _ in range(2)]
                for h_ in range(2):
                    r_ = slice(64 * h_, 64 * h_ + 64)
                    ro = slice(64 * (1 - h_), 64 * (1 - h_) + 64)
                    self.memset(qTm[h_][ro, :], 0.0)
                    self.act(qTm[h_][r_, :], pb[r_, :], ACT.Copy, scale=0.125)
                for j in range(q + 1):
                    pb = self.bank()
                    for k in range(8):
                        self.mm(pb[:], w[:, k, 128:256], self.HT[j][:, k, :], start=(k == 0), stop=(k == 7))
                    self.cp(kT[:, j * 512:(j + 1) * 512], pb[:], eng="act")
                    pb = self.bank()
                    for ci in range(4):
                        for k in range(8):
                            self.mm(pb[:, ci * 128:(ci + 1) * 128], self.HT[j][:, k, ci * 128:(ci + 1) * 128], w[:, k, 256:384],
                                    start=(k == 0), stop=(k == 7))
                    self.cp(v[:, j * 4:(j + 1) * 4, :], pb[:].rearrange("p (c d) -> p c d", c=4))
                Eb = [self.A("Eb", [128, 512]) for _ in range(2)]
                SPb = [[self.A("SPb", [128, 512], BF16) for _ in range(3)] for _ in range(2)]
                Wb = [[self.A("Wb", [128, 512], BF16) for _ in range(2)] for _ in range(2)]
                Lacc = [[self.A("Lacc", [128, 512], BF16) for _ in range(3)] for _ in range(2)]

                def head_gen(h):
                    r = slice(64 * h, 64 * h + 64)
                    pAs = [self.banks[4 * h], self.banks[4 * h + 1]]
                    pB = self.banks[4 * h + 2]
                    pC = self.banks[4 * h + 3]
                    E = Eb[h]
                    for i_ in range(3):
                        self.memset(Lacc[h][i_][:], 0.0)
                    self.mm(pC[:], v[:, 0, :], zb[:], start=True, stop=False)
                    blocks = list(range(nkb - 1, -1, -1))
                    nb = len(blocks)

                    def geom(n_):
                        kb = blocks[n_]
                        diag = kb >= 4 * q
                        cs = (kb - 4 * q) * 128 if diag else 0
                        return kb, diag, cs, slice(cs, 512), slice(kb * 128, (kb + 1) * 128)

                    def S1(n_):
                        kb, diag, cs, cr, ks = geom(n_)
                        SP, pA = SPb[h][n_ % 3], pAs[n_ % 2]
                        self.mm(pA[:, cr], kT[:, ks], qTm[h][:, cr])
                        self.act(E[:, cr], pA[:, cr], ACT.Exp)
                        self.act(SP[:, cr], E[:, cr], ACT.Ln, bias=1.0)
                        if diag:
                            self.tt(SP[:, cs:cs + 128], SP[:, cs:cs + 128], maskLT_b, ALU.mult)
                        if n_ + 1 < nb:
                            self.tt(Lacc[h][(n_ + 1) % 3][:, cr], Lacc[h][n_ % 3][:, cr], SP[:, cr], ALU.add)

                    def S2(n_):
                        kb, diag, cs, cr, ks = geom(n_)
                        SP, W_, La = SPb[h][n_ % 3], Wb[h][n_ % 2], Lacc[h][n_ % 3]
                        self.mm(pB[:, cr], kT[:, ks], qTm[h][:, cr], start=True, stop=False)
                        self.mm(pB[:, cr], negTri_b, SP[:, cr], start=False, stop=(n_ == 0))
                        if n_ > 0:
                            self.mm(pB[:, cr], negOnes_b, La[:, cr], start=False, stop=True)
                        self.act(W_[:, cr], pB[:, cr], ACT.Exp)
                        if diag:
                            self.tt(W_[:, cs:cs + 128], W_[:, cs:cs + 128], maskLT_b, ALU.mult)

                    def S3(n_):
                        kb, diag, cs, cr, ks = geom(n_)
                        W_ = Wb[h][n_ % 2]
                        self.mm(pC[:, cr], v[:, kb, :], W_[:, cr], start=False, stop=(kb == 0))

                    S1(0)
                    yield
                    if nb > 1:
                        S1(1)
                        yield
                    for i in range(nb + 1):
                        if i < nb:
                            S2(i)
                        if i + 2 < nb:
                            S1(i + 2)
                        if i >= 1:
                            S3(i - 1)
                        yield
                    self.cp(OA[r, hp, :], pC[r, :])

                gens = [head_gen(0), head_gen(1)]
                while gens:
                    for g_ in list(gens):
                        try:
                            next(g_)
                        except StopIteration:
                            gens.remove(g_)
            self.aoff = mark
            S.barrier()

        def merge_pass(l, q, OS, OA):
            mark = self.aoff
            S.barrier()
            load_gb(l, 0)
            merged = self.A("merged", [128, 4, D], BF16)
            m1 = self.A("m1", [128, 4, 512])
            sg = [self.A("sg", [128, 512]) for _ in range(2)]
            m2 = self.A("m2", [128, 512])
            for j in range(2):
                wba = self.wload(l, f"ba{j}")
                wga = self.wload(l, f"ga{j}")
                for ci in range(4):
                    csl = slice(ci * 128, (ci + 1) * 128)
                    pa, pg = self.bank(), self.bank()
                    for k in range(8):
                        self.mm(pa[:], OA[:, k, csl], wba[:, k, :], start=(k == 0), stop=(k == 7))
                    for k in range(8):
                        self.mm(pg[:], self.HT[q][:, k, csl], wga[:, k, :], start=(k == 0), stop=(k == 7))
                    s_ = sg[ci % 2]
                    self.act(s_[:], pg[:], ACT.Sigmoid)
                    self.tt(m1[:, ci, :], s_[:], pa[:], ALU.mult)
                wbsa = self.wload(l, f"bsa{j}")
                wbsb = self.wload(l, f"bsb{j}")
                wgs = self.wload(l, f"gs{j}")
                for ci in range(4):
                    csl = slice(ci * 128, (ci + 1) * 128)
                    pa, pg = self.bank(), self.bank()
                    for k in range(16):
                        ws_ = wbsa if k < 8 else wbsb
                        self.mm(pa[:], OS[:, k, csl], ws_[:, k % 8, :], start=(k == 0), stop=(k == 15))
                    for k in range(8):
                        self.mm(pg[:], self.HT[q][:, k, csl], wgs[:, k, :], start=(k == 0), stop=(k == 7))
                    s_ = sg[ci % 2]
                    self.act(s_[:], pg[:], ACT.Sigmoid)
                    self.tt(m2[:], s_[:], pa[:], ALU.mult)
                    self.tt(merged[:, ci, j * 512:(j + 1) * 512], m2[:], m1[:, ci, :], ALU.add)
            self.aoff = mark + 8192
            S.barrier()
            mT = self.A("mT", [128, 8, 512], BF16)
            for ci in range(4):
                pb = self.bank()
                pbb = pb[:].bitcast(BF16)
                for c in range(8):
                    self.tr(pbb[:, c * 128:(c + 1) * 128], merged[:, ci, c * 128:(c + 1) * 128], ident_b)
                self.cp(mT[:, :, ci * 128:(ci + 1) * 128], pbb.rearrange("p (c t) -> p c t", c=8), eng="act")
            wm = [self.wload(l, "mx0"), self.wload(l, "mx1")]
            for ci in range(4):
                csl = slice(ci * 128, (ci + 1) * 128)
                b2 = [self.bank(), self.bank()]
                for j in range(2):
                    for k in range(8):
                        self.mm(b2[j][:], mT[:, k, csl], wm[j][:, k, :], start=(k == 0), stop=(k == 7))
                postnorm_add(l, 0, 4 * q + ci, b2)
            self.aoff = mark
            S.barrier()

        def mixer(l):
            mark = self.aoff
            for q in range(NQ):
                self.aoff = mark
                S.barrier()
                OS = self.A("OS", [128, 16, 512], BF16)
                OA = self.A("OA", [128, 8, 512], BF16)
                if KSTOP >= 2:
                    ssd_pass(l, q, OS)
                if KSTOP >= 3:
                    attn_pass(l, q, OA)
                if KSTOP >= 4:
                    merge_pass(l, q, OS, OA)
            self.aoff = mark
            S.barrier()

        def xattn(l, b):
            mark = self.aoff
            S.barrier()
            load_gb(l, 1)
            kTm = self.A("kTm", [128, 8, 256], BF16)
            vm = self.A("vm", [128, 2, D], BF16)
            memT = self.A("memT", [128, 8, 256], BF16)
            memt = [self.A("memt", [128, D]) for _ in range(2)]
            for mt in range(2):
                self.dma("sp", memt[mt][:], mem_in[b, mt * 128:(mt + 1) * 128, :], memt[mt], after_bar=True)
            for mt in range(2):
                junk = self.A("junk", [128, D], BF16)
                mn = self.A("mn", [128, D], BF16)
                ss = self.stat()
                self.act(junk[:], memt[mt][:], ACT.Square, accum=ss)
                rs = self.rstd(ss, D)
                self.stt(mn[:], memt[mt][:], rs, GB[:], ALU.mult, ALU.mult)
                pb = self.bank()
                pbb = pb[:].bitcast(BF16)
                for c in range(8):
                    self.tr(pbb[:, c * 128:(c + 1) * 128], mn[:, c * 128:(c + 1) * 128], ident_b)
                self.cp(memT[:, :, mt * 128:(mt + 1) * 128], pbb.rearrange("p (c t) -> p c t", c=8))
            for j in range(2):
                wk = self.wload(l, f"xk{j}")
                for c in range(4):
                    pb = self.bank()
                    for k in range(8):
                        self.mm(pb[:, 0:256], wk[:, k, c * 128:(c + 1) * 128], memT[:, k, :], start=(k == 0), stop=(k == 7))
                    self.cp(kTm[:, j * 4 + c, :], pb[:, 0:256], eng="act")
            for j in range(2):
                wv = self.wload(l, f"xv{j}")
                for mt in range(2):
                    pb = self.bank()
                    for k in range(8):
                        self.mm(pb[:], memT[:, k, mt * 128:(mt + 1) * 128], wv[:, k, :], start=(k == 0), stop=(k == 7))
                    self.cp(vm[:, mt, j * 512:(j + 1) * 512], pb[:])
            load_gb(l, 2)
            self.aoff = mark + 8192 + 4096
            mark1 = self.aoff
            for hf in range(NQ // 2 if NQ >= 2 else 1):
                self.aoff = mark1
                S.barrier()
                qs = list(range(hf * 2, min(hf * 2 + 2, NQ)))
                prenorm(l, 1, [i for jq in qs for i in range(jq * 4, jq * 4 + 4)])
                OX = self.A("OX", [128, 8, 512 * len(qs)], BF16)
                qTh = [self.A("qTh", [128, 2, 512], BF16) for _ in range(2)]
                pT = [self.A("pT", [128, 2, 512], BF16) for _ in range(2)]
                rden = [self.A("rden", [128, 512]) for _ in range(2)]
                n_ = 0
                for hd in range(4):
                    wq = self.wload(l, f"xq{hd}")
                    for jj, jq in enumerate(qs):
                        qt, p_, rd = qTh[n_ % 2], pT[n_ % 2], rden[n_ % 2]
                        n_ += 1
                        for cc in range(2):
                            pb = self.bank()
                            for k in range(8):
                                self.mm(pb[:], wq[:, k, cc * 128:(cc + 1) * 128], self.HT[jq][:, k, :], start=(k == 0), stop=(k == 7))
                            self.act(qt[:, cc, :], pb[:], ACT.Copy, scale=1.0 / 16)
                        for mt in range(2):
                            pb = self.bank()
                            for cc in range(2):
                                self.mm(pb[:], kTm[:, hd * 2 + cc, mt * 128:(mt + 1) * 128], qt[:, cc, :], start=(cc == 0), stop=(cc == 1))
                            self.act(p_[:, mt, :], pb[:], ACT.Exp)
                        pd = self.bank()
                        for mt in range(2):
                            self.mm(pd[:], ones_b, p_[:, mt, :], start=(mt == 0), stop=(mt == 1))
                        self.act(rd[:], pd[:], ACT.Ln)
                        self.act(rd[:], rd[:], ACT.Exp, scale=-1.0)
                        for cc in range(2):
                            pb = self.bank()
                            for mt in range(2):
                                self.mm(pb[:], vm[:, mt, hd * 256 + cc * 128:hd * 256 + (cc + 1) * 128], p_[:, mt, :],
                                        start=(mt == 0), stop=(mt == 1))
                            self.tt(OX[:, hd * 2 + cc, jj * 512:(jj + 1) * 512], pb[:], rd[:], ALU.mult)
                wo = [self.wload(l, "xo0"), self.wload(l, "xo1")]
                for ti in range(4 * len(qs)):
                    csl = slice(ti * 128, (ti + 1) * 128)
                    b2 = [self.bank(), self.bank()]
                    for j in range(2):
                        for k in range(8):
                            self.mm(b2[j][:], OX[:, k, csl], wo[j][:, k, :], start=(k == 0), stop=(k == 7))
                    postnorm_add(l, 2, qs[0] * 4 + ti, b2)
            self.aoff = mark
            S.barrier()

        def ffn(l):
            mark = self.aoff
            S.barrier()
            load_gb(l, 3)
            for hf in range(max(NQ // 2, 1)):
                self.aoff = mark
                S.barrier()
                qs = list(range(hf * 2, min(hf * 2 + 2, NQ)))
                prenorm(l, 2, [i for jq in qs for i in range(jq * 4, jq * 4 + 4)])
                actT = self.A("actT", [128, 22, 512 * len(qs)], BF16)
                sgb = [self.A("sgb", [128, 512], BF16) for _ in range(2)]
                n_ = 0
                for fs in range(11):
                    w = self.wload(l, f"gu{fs}")
                    for fcc in range(2):
                        f = fs * 2 + fcc
                        for jj, jq in enumerate(qs):
                            pg, pu = self.bank(), self.bank()
                            for k in range(8):
                                self.mm(pg[:], w[:, k, fcc * 128:(fcc + 1) * 128], self.HT[jq][:, k, :], start=(k == 0), stop=(k == 7))
                            for k in range(8):
                                self.mm(pu[:], w[:, k, 256 + fcc * 128:256 + (fcc + 1) * 128], self.HT[jq][:, k, :], start=(k == 0), stop=(k == 7))
                            s_ = sgb[n_ % 2]
                            n_ += 1
                            self.act(s_[:], pg[:], ACT.Silu)
                            self.tt(actT[:, f, jj * 512:(jj + 1) * 512], s_[:], pu[:], ALU.mult)
                ntile = 4 * len(qs)
                for g0 in range(0, ntile, 4):
                    tiles = list(range(g0, min(g0 + 4, ntile)))
                    bks = {ti: [self.banks[2 * (ti - g0)], self.banks[2 * (ti - g0) + 1]] for ti in tiles}
                    for s in range(6):
                        wd = self.wload(l, f"dn{s}")
                        kcn = 4 if s < 5 else 2
                        for ti in tiles:
                            csl = slice(ti * 128, (ti + 1) * 128)
                            for kk in range(kcn):
                                f = s * 4 + kk
                                for j in range(2):
                                    self.mm(bks[ti][j][:], actT[:, f, csl], wd[:, kk, j * 512:(j + 1) * 512],
                                            start=(f == 0), stop=(f == 21))
                    for ti in tiles:
                        postnorm_add(l, 3, qs[0] * 4 + ti, bks[ti])
            self.aoff = mark
            S.barrier()

        for b in range(nseq):
            for i in range(NT):
                self.dma("sp", self.X[i][:], x_in[b, i * 128:(i + 1) * 128, :], self.X[i])
            for l in self.layers:
                mark = self.aoff
                if KSTOP >= 1:
                    prenorm(l, 0, range(NT))
                if KSTOP >= 2:
                    mixer(l)
                if KSTOP >= 5:
                    xattn(l, b)
                if KSTOP >= 6:
                    ffn(l)
                self.aoff = mark
            for i in range(NT):
                self.dma("sp", y_out[b, i * 128:(i + 1) * 128, :], self.X[i][:], self.X[i], store=True)
        S.emit(self.es)
        self.es.close()
        return nc


def host_inputs(inp, depth):
    wpack = np.stack([pack_layer(inp, l) for l in range(depth)])
    cst = make_consts().reshape(128, 8 * 128)
    gcols = np.zeros((128, depth * 24), np.float32)
    convT = np.zeros((128, depth * 120), np.float32)
    for l in range(depth):
        for w, nm in enumerate(("g_pre_mix", "g_pre_xa", "g_pre_ffn")):
            gcols[:, l * 24 + w * 8:l * 24 + (w + 1) * 8] = np.asarray(inp[nm][l]).reshape(8, 128).T
        cw = np.asarray(inp["conv_w"][l]).reshape(4, 24, 128)
        for k in range(4):
            convT[:, l * 120 + k:l * 120 + 96:4] = cw[k].T
        convT[:, l * 120 + 96:l * 120 + 120] = np.asarray(inp["conv_b"][l]).reshape(24, 128).T
    hparams = np.stack([np.asarray(inp["dt_bias"])[:depth].reshape(-1), np.asarray(inp["a_log"])[:depth].reshape(-1),
                        np.asarray(inp["d_skip"])[:depth].reshape(-1)]).astype(np.float32)
    grows = np.zeros((depth, 5, D), np.float32)
    for l in range(depth):
        grows[l, 0] = inp["g_post_mix"][l]
        grows[l, 1] = inp["g_mem"][l]
        grows[l, 2] = inp["g_post_xa"][l]
        grows[l, 3] = inp["g_post_ffn"][l]
    gssm = np.ascontiguousarray(np.asarray(inp["g_ssm_norm"])[:depth], dtype=np.float32)
    return dict(wpack=wpack, cst=cst, gcols=gcols, convT=convT, hparams=hparams, grows=grows, gssm=gssm)


def run(inp, n_cores, depth, trace=False):
    inp = {k: np.asarray(v, dtype=np.float32) for k, v in inp.items()}
    B, seq, _ = inp["x"].shape
    nseq = B // n_cores
    bld = Builder(nseq, seq, depth)
    nc = bld.build()
    shared = host_inputs(inp, depth)
    in_maps = []
    for c in range(n_cores):
        m = dict(shared)
        m["x"] = np.ascontiguousarray(inp["x"][c * nseq:(c + 1) * nseq])
        m["mem"] = np.ascontiguousarray(inp["mem"][c * nseq:(c + 1) * nseq])
        in_maps.append(m)
    res = run_bass_kernel_spmd(nc, in_maps, core_ids=list(range(n_cores)), trace=trace)
    out = np.concatenate([res.results[c]["y"] for c in range(n_cores)], axis=0)
    return out, res


def kernel(**inputs):
    out, _ = run(inputs, 8, 4)
    return out.astype(np.float32)
```

```python
import contextlib
import numpy as np
import concourse.bass as bass
import concourse.mybir as mybir
from concourse.bass_utils import run_bass_kernel_spmd

F32 = mybir.dt.float32
BF16 = mybir.dt.bfloat16
ACT = mybir.ActivationFunctionType
ALU = mybir.AluOpType

ENGS = ("pe", "act", "dve", "pool", "sp")
SEG = 20000
EPS = 1e-6


class Op:
    __slots__ = ("eng", "fn", "deps", "dma", "signal", "sigidx", "dmaval")

    def __init__(self, eng, fn):
        self.eng = eng
        self.fn = fn
        self.deps = []
        self.dma = None
        self.dmaval = 0
        self.signal = False
        self.sigidx = -1


class Sched:
    def __init__(self, nc):
        self.nc = nc
        self.ops = {e: [] for e in ENGS}
        self.last_w = {}
        self.readers = {}
        self.dma_cnt = {}
        self.bar = None
        self.bar_done = set()
        self.live = []

    def alloc(self, k, start, end):
        inherit, keep, seen = [], [], set()
        for (n, s_, e_) in self.live:
            if s_ < end and start < e_:
                w = self.last_w.get(n)
                for o in ([w] if w is not None else []) + list(self.readers.get(n, ())):
                    if id(o) not in seen:
                        seen.add(id(o))
                        inherit.append(o)
                if not (start <= s_ and e_ <= end):
                    keep.append((n, s_, e_))
            else:
                keep.append((n, s_, e_))
        keep.append((k, start, end))
        self.live = keep
        if inherit:
            self.readers[k] = inherit

    @staticmethod
    def key(ap):
        if isinstance(ap, str):
            return ap
        t = getattr(ap, "tensor", None)
        return t.name if t is not None else ap.name

    def barrier(self, engs=("pe", "act", "dve")):
        if NOBAR:
            return
        lasts = [self.ops[e][-1] for e in engs if self.ops[e]]
        self.bar = (engs, lasts)
        self.bar_done = set()

    def add(self, eng, fn, reads=(), writes=(), dma_key=None):
        op = Op(eng, fn)
        deps = []
        rk = [self.key(a) for a in reads if a is not None]
        wk = [self.key(a) for a in writes if a is not None]
        for k in rk:
            w = self.last_w.get(k)
            if w is not None:
                deps.append(w)
            if k.startswith("pb"):
                deps.extend(r for r in self.readers.get(k, ()) if r.eng != eng)
        for k in wk:
            w = self.last_w.get(k)
            if w is not None:
                deps.append(w)
            deps.extend(self.readers.get(k, ()))
        barlist = ()
        if self.bar is not None and eng in self.bar[0] and eng not in self.bar_done:
            barlist = self.bar[1]
            self.bar_done.add(eng)
        seen = set()
        for d in deps:
            if id(d) in seen:
                continue
            seen.add(id(d))
            if d.eng == "pe" and eng == "pe" and d.dma is None:
                continue
            op.deps.append(d)
        for d in barlist:
            if id(d) not in seen:
                seen.add(id(d))
                op.deps.append(d)
        for k in rk:
            self.readers.setdefault(k, []).append(op)
        for k in wk:
            self.last_w[k] = op
            self.readers[k] = []
        if dma_key is not None:
            op.dma = dma_key
            c = self.dma_cnt.get(dma_key, 0) + 1
            self.dma_cnt[dma_key] = c
            op.dmaval = 16 * c
        self.ops[eng].append(op)
        return op

    def emit(self, es):
        nc = self.nc
        for e in ENGS:
            for op in self.ops[e]:
                for d in op.deps:
                    if d.dma is None:
                        d.signal = True
        nsig = {}
        for e in ENGS:
            c = 0
            for op in self.ops[e]:
                if op.signal:
                    op.sigidx = c
                    c += 1
            nsig[e] = c
        sems = {}
        for e in ENGS:
            for s in range((nsig[e] + SEG - 1) // SEG):
                sems[(e, s)] = es.enter_context(nc.semaphore(f"s_{e}_{s}"))
        dsems = {}
        for k in self.dma_cnt:
            dsems[k] = es.enter_context(nc.semaphore("d_" + k))
        block = es.enter_context(nc.Block())
        final_waits = [(dsems[k], 16 * c) for k, c in self.dma_cnt.items()]

        def run(e, engobj):
            waited = {}
            for op in self.ops[e]:
                need = {}
                for d in op.deps:
                    if d.dma is not None:
                        sk = ("d", d.dma)
                        sem, val = dsems[d.dma], d.dmaval
                    else:
                        seg = d.sigidx // SEG
                        sk = (d.eng, seg)
                        sem, val = sems[sk], d.sigidx - seg * SEG + 1
                    if sk not in need or need[sk][1] < val:
                        need[sk] = (sem, val)
                for sk, (sem, val) in need.items():
                    if waited.get(sk, 0) >= val:
                        continue
                    waited[sk] = val
                    engobj.wait_ge(sem, val)
                ins = op.fn(engobj)
                if op.dma is not None:
                    ins.then_inc(dsems[op.dma], 16)
                elif op.signal:
                    ins.then_inc(sems[(e, op.sigidx // SEG)], 1)
            if e == "sp":
                for sem, val in final_waits:
                    engobj.wait_ge(sem, val)

        @block.tensor
        def _(t):
            run("pe", t)

        @block.scalar
        def _(t):
            run("act", t)

        @block.vector
        def _(t):
            run("dve", t)

        @block.gpsimd
        def _(t):
            run("pool", t)

        @block.sync
        def _(t):
            run("sp", t)


import os
KSTOP = int(os.environ.get("KSTOP", "9"))
KSUB = int(os.environ.get("KSUB", "99"))
KSS = int(os.environ.get("KSS", "99"))
CONV_ENG = os.environ.get("CONV_ENG", "dve")
NOBAR = int(os.environ.get("NOBAR", "1"))
D = 1024
NSLOT = 3
SLOT_ELEMS = 4096


def weight_slices():
    sl = {}
    sl["dt"] = (8, 32)
    for g in range(4):
        sl[f"x{g}"] = (8, 512)
        sl[f"bc{g}"] = (8, 256)
        sl[f"z{g}"] = (8, 512)
    for hp in range(8):
        sl[f"qkv{hp}"] = (8, 384)
    for j in range(2):
        for nm in ("ba", "ga", "gs", "bsa", "bsb", "mx", "xk", "xv", "xo"):
            sl[f"{nm}{j}"] = (8, 512)
    for hd in range(4):
        sl[f"xq{hd}"] = (8, 256)
    for fs in range(11):
        sl[f"gu{fs}"] = (8, 512)
    for s in range(5):
        sl[f"dn{s}"] = (4, 1024)
    sl["dn5"] = (2, 1024)
    offs = {}
    o = 0
    for k, (kc, n) in sl.items():
        offs[k] = (o, kc, n)
        o += kc * n
    return offs, o


def pack_layer(inp, l):
    offs, tot = weight_slices()
    out = np.empty((128, tot), np.float32)

    def put(name, W):
        o, kc, n = offs[name]
        assert W.shape == (kc * 128, n), (name, W.shape)
        out[:, o:o + kc * n] = W.reshape(kc, 128, n).transpose(1, 0, 2).reshape(128, kc * n)

    w_in = inp["w_in"][l]
    put("dt", w_in[:, 8192:8224])
    for g in range(4):
        put(f"x{g}", w_in[:, 5120 + g * 512:5120 + (g + 1) * 512])
        put(f"bc{g}", np.concatenate([w_in[:, 7168 + g * 128:7168 + (g + 1) * 128],
                                      w_in[:, 7680 + g * 128:7680 + (g + 1) * 128]], axis=1))
        put(f"z{g}", w_in[:, 3072 + g * 512:3072 + (g + 1) * 512])
    for hp in range(8):
        put(f"qkv{hp}", np.concatenate([w_in[:, hp * 128:(hp + 1) * 128],
                                        w_in[:, 1024 + hp * 128:1024 + (hp + 1) * 128],
                                        w_in[:, 2048 + hp * 128:2048 + (hp + 1) * 128]], axis=1))
    for j in range(2):
        c = slice(j * 512, (j + 1) * 512)
        put(f"ba{j}", inp["w_br_att"][l][:, c])
        put(f"ga{j}", w_in[:, 8224 + j * 512:8224 + (j + 1) * 512])
        put(f"gs{j}", w_in[:, 9248 + j * 512:9248 + (j + 1) * 512])
        put(f"bsa{j}", inp["w_br_ssm"][l][0:1024, c])
        put(f"bsb{j}", inp["w_br_ssm"][l][1024:2048, c])
        put(f"mx{j}", inp["w_mix_out"][l][:, c])
        put(f"xk{j}", inp["w_xkv"][l][:, j * 512:(j + 1) * 512])
        put(f"xv{j}", inp["w_xkv"][l][:, 1024 + j * 512:1024 + (j + 1) * 512])
        put(f"xo{j}", inp["w_xo"][l][:, c])
    for hd in range(4):
        put(f"xq{hd}", inp["w_xq"][l][:, hd * 256:(hd + 1) * 256])
    wgu = inp["w_gu"][l]
    for fs in range(11):
        put(f"gu{fs}", np.concatenate([wgu[:, fs * 256:(fs + 1) * 256],
                                       wgu[:, 2816 + fs * 256:2816 + (fs + 1) * 256]], axis=1))
    wd = inp["w_down"][l]
    for s in range(5):
        put(f"dn{s}", wd[s * 512:(s + 1) * 512])
    put("dn5", wd[2560:2816])
    return out


def make_consts():
    j = np.arange(128)[:, None]
    l = np.arange(128)[None, :]
    c = np.zeros((128, 8, 128), np.float32)
    c[:, 0] = (j == l)
    c[:, 1] = (j <= l)
    c[:, 2] = (j > l)
    c[:, 3] = 1.0
    c[:, 4] = (j < l)
    c[:, 5] = -(j >= l).astype(np.float32)
    c[:, 6] = -1.0
    c[:, 7] = 0.0
    return c


class Builder:
    def __init__(self, nseq, seq, depth, layers=None):
        self.nseq, self.seq, self.depth = nseq, seq, depth
        self.NT = seq // 128
        self.NQ = seq // 512
        self.layers = list(range(depth)) if layers is None else layers
        self.offs, self.wtot = weight_slices()
        nc = bass.Bass("TRN2", target_bir_lowering=False)
        self.nc = nc
        self.S = Sched(nc)
        self.es = contextlib.ExitStack()
        self.uid = 0

    def sb(self, name, shape, dt=F32):
        return self.nc.alloc_sbuf_tensor("s_" + name, shape, dt)

    def A(self, name, shape, dt=F32):
        self.uid += 1
        nb = int(np.prod(shape[1:])) * (4 if dt == F32 else 2)
        nb = (nb + 31) // 32 * 32
        t = self.nc.alloc_sbuf_tensor_at(f"{name}{self.uid}", shape, dt, offset=self.aoff)
        self.S.alloc(self.S.key(t[:]), self.aoff, self.aoff + nb)
        self.aoff += nb
        assert self.aoff <= self.aend, (name, self.aoff, self.aend)
        return t

    def _aps(self, *xs):
        return [x for x in xs if x is not None and not isinstance(x, (int, float))]

    def mm(self, out, lhsT, rhs, start=True, stop=True):
        self.S.add("pe", lambda e: e.matmul(out, lhsT, rhs, start=start, stop=stop), reads=[lhsT, rhs], writes=[out])

    def tr(self, out, in_, ident):
        self.S.add("pe", lambda e: e.transpose(out, in_, ident), reads=[in_, ident], writes=[out])

    def act(self, out, in_, func, bias=None, scale=None, accum=None):
        kw = {}
        if bias is not None:
            kw["bias"] = bias
        if scale is not None:
            kw["scale"] = scale
        if accum is not None:
            kw["accum_out"] = accum
        self.S.add("act", lambda e: e.activation(out=out, in_=in_, func=func, **kw),
                   reads=self._aps(in_, bias, scale), writes=self._aps(out, accum))

    def tt(self, out, in0, in1, op, eng="dve"):
        self.S.add(eng, lambda e: e.tensor_tensor(out=out, in0=in0, in1=in1, op=op), reads=[in0, in1], writes=[out])

    def ts(self, out, in0, s1, s2, op0, op1=None, eng="dve"):
        if op1 is None:
            fn = lambda e: e.tensor_scalar(out=out, in0=in0, scalar1=s1, scalar2=None, op0=op0)
        else:
            fn = lambda e: e.tensor_scalar(out=out, in0=in0, scalar1=s1, scalar2=s2, op0=op0, op1=op1)
        self.S.add(eng, fn, reads=self._aps(in0, s1, s2), writes=[out])

    def stt(self, out, in0, scalar, in1, op0, op1, eng="dve"):
        self.S.add(eng, lambda e: e.scalar_tensor_tensor(out=out, in0=in0, scalar=scalar, in1=in1, op0=op0, op1=op1),
                   reads=self._aps(in0, scalar, in1), writes=[out])

    def cp(self, out, in_, eng="dve"):
        if eng == "act":
            self.S.add("act", lambda e: e.copy(out=out, in_=in_), reads=[in_], writes=[out])
        else:
            self.S.add(eng, lambda e: e.tensor_copy(out=out, in_=in_), reads=[in_], writes=[out])

    def memset(self, ap, val, eng="dve"):
        self.S.add(eng, lambda e: e.memset(ap, val), writes=[ap])

    def recip(self, out, in_):
        self.S.add("dve", lambda e: e.reciprocal(out=out, in_=in_), reads=[in_], writes=[out])

    def dma(self, eng, out, in_, sbuf_side, after_bar=False, store=False):
        k = self.S.key(sbuf_side)
        if not store:
            op = self.S.add(eng, lambda e: e.dma_start(out=out, in_=in_), writes=[out], dma_key=k)
        else:
            op = self.S.add(eng, lambda e: e.dma_start(out=out, in_=in_), reads=[in_], dma_key=k)
        if after_bar and self.S.bar is not None:
            for d in self.S.bar[1]:
                if d not in op.deps:
                    op.deps.append(d)

    def wload(self, l, name):
        o, kc, n = self.offs[name]
        slot = self.wslots[self.wnext % NSLOT]
        self.wnext += 1
        dst = slot[:, 0:kc * n]
        src = self.wpack[l, :, o:o + kc * n]
        self.S.add("pool", lambda e: e.dma_start(out=dst, in_=src), writes=[slot], dma_key=self.S.key(slot))
        return slot[:, 0:kc * n].rearrange("p (k n) -> p k n", k=kc)

    def rstd(self, ss, n):
        rs = self.stat()
        self.act(rs, ss, ACT.Ln, bias=self.epsb[:, 0:1], scale=1.0 / n)
        self.act(rs, rs, ACT.Exp, scale=-0.5)
        return rs

    def stat(self, w=1):
        t = self.stats[self.statn % len(self.stats)]
        self.statn += 1
        return t[:, 0:w]

    def bank(self):
        b = self.banks[self.bankn % 6]
        self.bankn += 1
        return b

    def build(self):
        nc, S = self.nc, self.S
        NT, NQ, nseq, seq = self.NT, self.NQ, self.nseq, self.seq
        DEPTH = self.depth
        x_in = nc.dram_tensor("x", [nseq, seq, D], F32, kind="ExternalInput").ap()
        mem_in = nc.dram_tensor("mem", [nseq, 256, D], F32, kind="ExternalInput").ap()
        self.wpack = nc.dram_tensor("wpack", [DEPTH, 128, self.wtot], F32, kind="ExternalInput").ap()
        cst_in = nc.dram_tensor("cst", [128, 8 * 128], F32, kind="ExternalInput").ap()
        gcols_in = nc.dram_tensor("gcols", [128, DEPTH * 24], F32, kind="ExternalInput").ap()
        convT_in = nc.dram_tensor("convT", [128, DEPTH * 120], F32, kind="ExternalInput").ap()
        hp_in = nc.dram_tensor("hparams", [3, DEPTH * 32], F32, kind="ExternalInput").ap()
        grow_in = nc.dram_tensor("grows", [DEPTH, 5, D], F32, kind="ExternalInput").ap()
        gssm_in = nc.dram_tensor("gssm", [DEPTH, 2048], F32, kind="ExternalInput").ap()
        y_out = nc.dram_tensor("y", [nseq, seq, D], F32, kind="ExternalOutput").ap()

        self.X = [self.sb(f"X{i}", [128, D]) for i in range(NT)]
        self.HT = [self.sb(f"HT{j}", [128, 8, 512], BF16) for j in range(NQ)]
        self.wslots = [self.sb(f"ws{i}", [128, SLOT_ELEMS], BF16) for i in range(NSLOT)]
        self.wnext = 0
        cf = self.sb("cf", [128, 8, 128])
        cb = self.sb("cb", [128, 8, 128], BF16)
        gcols = self.sb("gcols", [128, DEPTH * 24])
        convT = self.sb("convT", [128, DEPTH * 120])
        dtb = self.sb("dtb", [128, DEPTH * 32])
        alog = self.sb("alog", [128, DEPTH * 32])
        aneg = self.sb("aneg", [128, DEPTH * 32])
        dsk = self.sb("dsk", [128, DEPTH * 32])
        self.epsb = self.sb("epsb", [128, 1])
        GB = self.sb("GB", [128, D])
        GS = self.sb("GS", [128, 512])
        Sst = [self.sb(f"Sst{g}", [128, 512]) for g in range(4)]
        halo = self.sb("halo", [128, 24, 3], BF16)
        self.stats = [self.sb(f"st{i}", [128, 2]) for i in range(16)]
        self.statn = 0
        zb = self.sb("zb", [128, 512], BF16)
        pj = self.sb("pj", [128, 512], BF16)
        ptmp = [self.sb(f"ptmp{i}", [128, 512]) for i in range(2)]
        self.banks = [nc.alloc_psum_tensor(f"pb{i}", [128, 512], F32) for i in range(8)]
        self.bankn = 0
        self.aoff0 = (nc._sbuf_addr_for_side("left") + 63) // 64 * 64
        self.aend = nc._sbuf_addr_for_side("right")
        self.aoff = self.aoff0
        ident_b, triLE_f, sGT_f, ones_f, ones_b = cb[:, 0, :], cf[:, 1, :], cf[:, 2, :], cf[:, 3, :], cb[:, 3, :]
        maskLT_b, negTri_b, negOnes_b = cb[:, 4, :], cb[:, 5, :], cb[:, 6, :]

        self.dma("sp", cf[:].rearrange("p a b -> p (a b)"), cst_in[:, :], cf)
        self.dma("sp", gcols[:], gcols_in[:, :], gcols)
        self.dma("sp", convT[:], convT_in[:, :], convT)
        self.dma("sp", dtb[:], hp_in[0:1, :].partition_broadcast(128), dtb)
        self.dma("sp", alog[:], hp_in[1:2, :].partition_broadcast(128), alog)
        self.dma("sp", dsk[:], hp_in[2:3, :].partition_broadcast(128), dsk)
        self.cp(cb[:], cf[:])
        self.memset(self.epsb[:], EPS)
        self.memset(zb[:], 0.0)
        self.act(aneg[:], alog[:], ACT.Exp)
        self.ts(aneg[:], aneg[:], -1.0, None, ALU.mult)

        def prenorm(l, which, tiles):
            goff = l * 24 + which * 8
            mark = self.aoff
            junks = [self.A("junk", [128, D], BF16) for _ in range(2)]
            xns = [self.A("xn", [128, D], BF16) for _ in range(2)]
            for n_, i in enumerate(tiles):
                j, ci = i // 4, i % 4
                junk, xn = junks[n_ % 2], xns[n_ % 2]
                ss = self.stat()
                self.act(junk[:], self.X[i][:], ACT.Square, accum=ss)
                rs = self.rstd(ss, D)
                self.ts(xn[:], self.X[i][:], rs, None, ALU.mult)
                pb = self.bank()
                pbb = pb[:].bitcast(BF16)
                for c in range(8):
                    self.tr(pbb[:, c * 128:(c + 1) * 128], xn[:, c * 128:(c + 1) * 128], ident_b)
                self.tt(self.HT[j][:, :, ci * 128:(ci + 1) * 128],
                        pbb.rearrange("p (c t) -> p c t", c=8),
                        gcols[:, goff:goff + 8].unsqueeze(2).to_broadcast([128, 8, 128]), ALU.mult)
            self.aoff = mark
            S.barrier()

        def postnorm_add(l, grow, i, banks2):
            ss2 = self.stat(2)
            junk = pj
            for b in range(2):
                self.act(junk[:], banks2[b][:], ACT.Square, accum=ss2[:, b:b + 1])
            ss = self.stat()
            self.tt(ss, ss2[:, 0:1], ss2[:, 1:2], ALU.add)
            rs = self.rstd(ss, D)
            for b in range(2):
                tmp = ptmp[b]
                self.stt(tmp[:], banks2[b][:], rs, GB[:, b * 512:(b + 1) * 512], ALU.mult, ALU.mult)
                self.tt(self.X[i][:, b * 512:(b + 1) * 512], self.X[i][:, b * 512:(b + 1) * 512], tmp[:], ALU.add)

        def load_gb(l, which):
            self.dma("sp", GB[:], grow_in[l, which:which + 1, :].partition_broadcast(128), GB)

        def ssd_pass(l, q, OS):
            mark = self.aoff
            wdt = self.wload(l, "dt")
            dt_all = self.A("dt_all", [128, 4, 32])
            adt_all = self.A("adt_all", [128, 4, 32])
            pb = self.bank()
            for ci in range(4):
                for k in range(8):
                    self.mm(pb[:, ci * 32:(ci + 1) * 32], self.HT[q][:, k, ci * 128:(ci + 1) * 128], wdt[:, k, :],
                            start=(k == 0), stop=(k == 7))
            self.tt(dt_all[:], pb[:, 0:128].rearrange("p (c h) -> p c h", c=4),
                    dtb[:, l * 32:(l + 1) * 32].unsqueeze(1).to_broadcast([128, 4, 32]), ALU.add)
            self.act(dt_all[:], dt_all[:], ACT.Exp)
            self.act(dt_all[:], dt_all[:], ACT.Ln, bias=1.0)
            self.tt(adt_all[:], dt_all[:], aneg[:, l * 32:(l + 1) * 32].unsqueeze(1).to_broadcast([128, 4, 32]), ALU.mult)
            acs_all = self.A("acs_all", [128, 256])
            eacs_all = self.A("eacs_all", [128, 4, 32])
            dte_all = self.A("dte_all", [128, 4, 32])
            cd_all = self.A("cd_all", [128, 4, 32])
            pa = self.bank()
            adt_flat = adt_all[:].rearrange("p c h -> p (c h)")
            self.mm(pa[:, 0:128], triLE_f, adt_flat)
            self.mm(pa[:, 128:256], ones_f, adt_flat)
            self.cp(acs_all[:], pa[:, 0:256])
            self.tt(dte_all[:].rearrange("p c h -> p (c h)"), acs_all[:, 128:256], acs_all[:, 0:128], ALU.subtract)
            self.act(eacs_all[:].rearrange("p c h -> p (c h)"), acs_all[:, 0:128], ACT.Exp)
            self.act(dte_all[:].rearrange("p c h -> p (c h)"), dte_all[:].rearrange("p c h -> p (c h)"), ACT.Exp)
            self.act(cd_all[:].rearrange("p c h -> p (c h)"), acs_all[:, 128:256], ACT.Exp)
            mark2 = self.aoff
            if KSUB < 1:
                return
            for g in range(4):
                self.aoff = mark2
                S.barrier()
                self.dma("sp", GS[:], gssm_in[l:l + 1, g * 512:(g + 1) * 512].partition_broadcast(128), GS)
                wx = self.wload(l, f"x{g}")
                wbc = self.wload(l, f"bc{g}")
                xT = self.A("xT", [128, 4, 512], BF16)
                BT = self.A("BT", [128, 512], BF16)
                CT = self.A("CT", [128, 512], BF16)
                Sb = self.A("Sb", [128, 512], BF16)
                if q == 0:
                    self.memset(Sst[g][:], 0.0)
                    self.memset(halo[:, g * 6:(g + 1) * 6, :], 0.0)
                self.cp(Sb[:], Sst[g][:], eng="act")
                raws = [self.A("raw", [128, 516], BF16) for _ in range(3)]
                dgs = [self.A("dg", [128, 4, 128], BF16) for _ in range(6)]

                def chunk_of(fc):
                    return g * 4 + fc if fc < 4 else (16 + g if fc == 4 else 20 + g)

                for fc in range(6):
                    col = l * 120 + chunk_of(fc) * 4
                    self.tt(dgs[fc][:], cf[:, 0, :].unsqueeze(1).to_broadcast([128, 4, 128]),
                            convT[:, col:col + 4].unsqueeze(2).to_broadcast([128, 4, 128]), ALU.mult)
                pbs = {}

                def proj(fc):
                    pb = self.bank()
                    pbs[fc] = pb
                    for k in range(8):
                        if fc < 4:
                            lhsT = wx[:, k, fc * 128:(fc + 1) * 128]
                        else:
                            lhsT = wbc[:, k, (fc - 4) * 128:(fc - 3) * 128]
                        self.mm(pb[:], lhsT, self.HT[q][:, k, :], start=(k == 0), stop=(k == 7))

                def post(fc):
                    raw = raws[fc % 3]
                    pb = pbs[fc]
                    chunk = chunk_of(fc)
                    hl = halo[:, g * 6 + fc, :]
                    self.cp(raw[:, 0:3], hl)
                    self.cp(raw[:, 3:515], pb[:], eng="act")
                    self.cp(hl, raw[:, 512:515])
                    pcv = self.bank()
                    for kk in range(4):
                        self.mm(pcv[:], dgs[fc][:, kk, :], raw[:, kk:kk + 512], start=(kk == 0), stop=(kk == 3))
                    dst = xT[:, fc, :] if fc < 4 else (BT[:] if fc == 4 else CT[:])
                    self.act(dst, pcv[:], ACT.Silu, bias=convT[:, l * 120 + 96 + chunk:l * 120 + 96 + chunk + 1])

                proj(0)
                proj(1)
                for fc in range(6):
                    post(fc)
                    if fc + 2 < 6:
                        proj(fc + 2)
                if KSUB < 2:
                    continue
                wz = self.wload(l, f"z{g}")
                self.aoff -= 3 * 1056 + 6144
                S.barrier()
                zs_all = self.A("zs_all", [128, 4, 512])
                for ci in range(4):
                    pz = self.bank()
                    for k in range(8):
                        self.mm(pz[:], self.HT[q][:, k, ci * 128:(ci + 1) * 128], wz[:, k, :], start=(k == 0), stop=(k == 7))
                    self.act(zs_all[:, ci, :], pz[:], ACT.Silu)
                xdts = [self.A("xdt", [128, 512], BF16) for _ in range(2)]
                Btoks = [self.A("Btok", [128, 128], BF16) for _ in range(2)]
                xtok = self.A("xtok", [128, 512], BF16)
                R = self.A("R", [128, 4, 128])
                Dm = self.A("Dm", [128, 8, 128], BF16)
                cbm = self.A("cbm", [128, 128], BF16)
                xD = self.A("xD", [128, 512], BF16)
                ytmp = self.A("ytmp", [128, 512])
                junk = self.A("junk", [128, 512], BF16)
                ob = self.A("ob", [128, 512], BF16)
                xdte = self.A("xdte", [128, 512], BF16)
                dskg = dsk[:, l * 32 + g * 8:l * 32 + (g + 1) * 8]

                def genA(ci):
                    c0 = ci * 128
                    csl = slice(c0, c0 + 128)
                    dt_g = dt_all[:, ci, g * 8:(g + 1) * 8]
                    adt_g = adt_all[:, ci, g * 8:(g + 1) * 8]
                    xdt, Btok = xdts[ci % 2], Btoks[ci % 2]
                    py = self.banks[6 + ci % 2]
                    pb = self.bank()
                    pbb = pb[:].bitcast(BF16)
                    for fc in range(4):
                        self.tr(pbb[:, fc * 128:(fc + 1) * 128], xT[:, fc, csl], ident_b)
                    self.tr(pbb[:, 512:640], BT[:, csl], ident_b)
                    self.cp(xtok[:], pbb[:, 0:512], eng="act")
                    self.cp(Btok[:], pbb[:, 512:640])
                    yield
                    for hh in range(2):
                        self.tt(R[:], triLE_f.unsqueeze(1).to_broadcast([128, 4, 128]),
                                adt_g[:, hh * 4:(hh + 1) * 4].unsqueeze(2).to_broadcast([128, 4, 128]), ALU.mult)
                        pe_ = self.bank()
                        self.mm(pe_[:], sGT_f, R[:].rearrange("p a b -> p (a b)"))
                        self.act(Dm[:, hh * 4:(hh + 1) * 4, :].rearrange("p a b -> p (a b)"), pe_[:], ACT.Exp)
                        yield
                    pc = self.bank()
                    self.mm(pc[:, 0:128], BT[:, csl], CT[:, csl])
                    self.tt(cbm[:], pc[:, 0:128], triLE_f, ALU.mult)
                    self.tt(Dm[:], Dm[:], cbm[:].unsqueeze(1).to_broadcast([128, 8, 128]), ALU.mult)
                    yield
                    x3 = xtok[:].rearrange("p (h d) -> p h d", h=8)
                    self.tt(xdt[:].rearrange("p (h d) -> p h d", h=8), x3, dt_g.unsqueeze(2).to_broadcast([128, 8, 64]), ALU.mult)
                    self.tt(xD[:].rearrange("p (h d) -> p h d", h=8), x3, dskg.unsqueeze(2).to_broadcast([128, 8, 64]), ALU.mult)
                    yield
                    for h in range(8):
                        self.mm(py[:, h * 64:(h + 1) * 64], Dm[:, h, :], xdt[:, h * 64:(h + 1) * 64], start=(h == 0), stop=False)
                    self.mm(py[:], ident_b, xD[:], start=False, stop=True)
                    yield

                def genB(ci):
                    c0 = ci * 128
                    csl = slice(c0, c0 + 128)
                    xdt, Btok = xdts[ci % 2], Btoks[ci % 2]
                    zs = zs_all[:, ci, :]
                    gsl = slice(g * 8, (g + 1) * 8)
                    py = self.banks[6 + ci % 2]
                    po = self.bank()
                    self.mm(po[:], CT[:, csl], Sb[:])
                    self.tt(ytmp[:].rearrange("p (h d) -> p h d", h=8), po[:].rearrange("p (h d) -> p h d", h=8),
                            eacs_all[:, ci, gsl].unsqueeze(2).to_broadcast([128, 8, 64]), ALU.mult)
                    self.tt(ytmp[:], ytmp[:], py[:], ALU.add)
                    yield
                    self.tt(ytmp[:], ytmp[:], zs, ALU.mult)
                    ss = self.stat()
                    self.act(junk[:], ytmp[:], ACT.Square, accum=ss)
                    rs = self.rstd(ss, 512)
                    self.stt(ob[:], ytmp[:], rs, GS[:], ALU.mult, ALU.mult)
                    yield
                    pt = self.bank()
                    ptb = pt[:].bitcast(BF16)
                    for fc in range(4):
                        self.tr(ptb[:, fc * 128:(fc + 1) * 128], ob[:, fc * 128:(fc + 1) * 128], ident_b)
                    self.cp(OS[:, g * 4:(g + 1) * 4, csl], ptb[:, 0:512].rearrange("p (c t) -> p c t", c=4), eng="act")
                    yield
                    self.tt(xdte[:].rearrange("p (h d) -> p h d", h=8), xdt[:].rearrange("p (h d) -> p h d", h=8),
                            dte_all[:, ci, gsl].unsqueeze(2).to_broadcast([128, 8, 64]), ALU.mult)
                    psn = self.bank()
                    self.mm(psn[:], Btok[:], xdte[:])
                    self.tt(Sst[g][:].rearrange("p (h d) -> p h d", h=8), Sst[g][:].rearrange("p (h d) -> p h d", h=8),
                            cd_all[:, ci, gsl].unsqueeze(2).to_broadcast([128, 8, 64]), ALU.mult)
                    self.tt(Sst[g][:], Sst[g][:], psn[:], ALU.add)
                    self.cp(Sb[:], Sst[g][:], eng="act")
                    yield

                def run_gens(gens):
                    gens = list(gens)
                    while gens:
                        for g_ in list(gens):
                            try:
                                next(g_)
                            except StopIteration:
                                gens.remove(g_)

                run_gens([genA(0)])
                for ci in range(4):
                    run_gens([genB(ci)] + ([genA(ci + 1)] if ci < 3 else []))
            self.aoff = mark
            S.barrier()

        def attn_pass(l, q, OA):
            mark = self.aoff
            nkb = 4 * (q + 1)
            for hp in range(8):
                self.aoff = mark
                S.barrier()
                w = self.wload(l, f"qkv{hp}")
                kT = self.A("kT", [128, (q + 1) * 512], BF16)
                v = self.A("v", [128, nkb, 128], BF16)
                pb = self.bank()
                for k in range(8):
                    self.mm(pb[:], w[:, k, 0:128], self.HT[q][:, k, :], start=(k == 0), stop=(k == 7))
                qTm = [self.A("qTm", [128, 512], BF16) for _ in range(2)]
                for h_ in range(2):
                    r_ = slice(64 * h_, 64 * h_ + 64)
                    ro = slice(64 * (1 - h_), 64 * (1 - h_) + 64)
                    self.memset(qTm[h_][ro, :], 0.0)
                    self.act(qTm[h_][r_, :], pb[r_, :], ACT.Copy, scale=0.125)
                for j in range(q + 1):
                    pb = self.bank()
                    for k in range(8):
                        self.mm(pb[:], w[:, k, 128:256], self.HT[j][:, k, :], start=(k == 0), stop=(k == 7))
                    self.cp(kT[:, j * 512:(j + 1) * 512], pb[:], eng="act")
                    pb = self.bank()
                    for ci in range(4):
                        for k in range(8):
                            self.mm(pb[:, ci * 128:(ci + 1) * 128], self.HT[j][:, k, ci * 128:(ci + 1) * 128], w[:, k, 256:384],
                                    start=(k == 0), stop=(k == 7))
                    self.cp(v[:, j * 4:(j + 1) * 4, :], pb[:].rearrange("p (c d) -> p c d", c=4))
                Eb = [self.A("Eb", [128, 512]) for _ in range(2)]
                SPb = [[self.A("SPb", [128, 512], BF16) for _ in range(3)] for _ in range(2)]
                Wb = [[self.A("Wb", [128, 512], BF16) for _ in range(2)] for _ in range(2)]
                Lacc = [[self.A("Lacc", [128, 512], BF16) for _ in range(3)] for _ in range(2)]

                def head_gen(h):
                    r = slice(64 * h, 64 * h + 64)
                    pAs = [self.banks[4 * h], self.banks[4 * h + 1]]
                    pB = self.banks[4 * h + 2]
                    pC = self.banks[4 * h + 3]
                    E = Eb[h]
                    for i_ in range(3):
                        self.memset(Lacc[h][i_][:], 0.0)
                    self.mm(pC[:], v[:, 0, :], zb[:], start=True, stop=False)
                    blocks = list(range(nkb - 1, -1, -1))
                    nb = len(blocks)

                    def geom(n_):
                        kb = blocks[n_]
                        diag = kb >= 4 * q
                        cs = (kb - 4 * q) * 128 if diag else 0
                        return kb, diag, cs, slice(cs, 512), slice(kb * 128, (kb + 1) * 128)

                    def S1(n_):
                        kb, diag, cs, cr, ks = geom(n_)
                        SP, pA = SPb[h][n_ % 3], pAs[n_ % 2]
                        self.mm(pA[:, cr], kT[:, ks], qTm[h][:, cr])
                        self.act(E[:, cr], pA[:, cr], ACT.Exp)
                        self.act(SP[:, cr], E[:, cr], ACT.Ln, bias=1.0)
                        if diag:
                            self.tt(SP[:, cs:cs + 128], SP[:, cs:cs + 128], maskLT_b, ALU.mult)
                        if n_ + 1 < nb:
                            self.tt(Lacc[h][(n_ + 1) % 3][:, cr], Lacc[h][n_ % 3][:, cr], SP[:, cr], ALU.add)

                    def S2(n_):
                        kb, diag, cs, cr, ks = geom(n_)
                        SP, W_, La = SPb[h][n_ % 3], Wb[h][n_ % 2], Lacc[h][n_ % 3]
                        self.mm(pB[:, cr], kT[:, ks], qTm[h][:, cr], start=True, stop=False)
                        self.mm(pB[:, cr], negTri_b, SP[:, cr], start=False, stop=(n_ == 0))
                        if n_ > 0:
                            self.mm(pB[:, cr], negOnes_b, La[:, cr], start=False, stop=True)
                        self.act(W_[:, cr], pB[:, cr], ACT.Exp)
                        if diag:
                            self.tt(W_[:, cs:cs + 128], W_[:, cs:cs + 128], maskLT_b, ALU.mult)

                    def S3(n_):
                        kb, diag, cs, cr, ks = geom(n_)
                        W_ = Wb[h][n_ % 2]
                        self.mm(pC[:, cr], v[:, kb, :], W_[:, cr], start=False, stop=(kb == 0))

                    S1(0)
                    yield
                    if nb > 1:
                        S1(1)
                        yield
                    for i in range(nb + 1):
                        if i < nb:
                            S2(i)
                        if i + 2 < nb:
                            S1(i + 2)
                        if i >= 1:
                            S3(i - 1)
                        yield
                    self.cp(OA[r, hp, :], pC[r, :])

                gens = [head_gen(0), head_gen(1)]
                while gens:
                    for g_ in list(gens):
                        try:
                            next(g_)
                        except StopIteration:
                            gens.remove(g_)
            self.aoff = mark
            S.barrier()

        def merge_pass(l, q, OS, OA):
            mark = self.aoff
            S.barrier()
            load_gb(l, 0)
            merged = self.A("merged", [128, 4, D], BF16)
            m1 = self.A("m1", [128, 4, 512])
            sg = [self.A("sg", [128, 512]) for _ in range(2)]
            m2 = self.A("m2", [128, 512])
            for j in range(2):
                wba = self.wload(l, f"ba{j}")
                wga = self.wload(l, f"ga{j}")
                for ci in range(4):
                    csl = slice(ci * 128, (ci + 1) * 128)
                    pa, pg = self.bank(), self.bank()
                    for k in range(8):
                        self.mm(pa[:], OA[:, k, csl], wba[:, k, :], start=(k == 0), stop=(k == 7))
                    for k in range(8):
                        self.mm(pg[:], self.HT[q][:, k, csl], wga[:, k, :], start=(k == 0), stop=(k == 7))
                    s_ = sg[ci % 2]
                    self.act(s_[:], pg[:], ACT.Sigmoid)
                    self.tt(m1[:, ci, :], s_[:], pa[:], ALU.mult)
                wbsa = self.wload(l, f"bsa{j}")
                wbsb = self.wload(l, f"bsb{j}")
                wgs = self.wload(l, f"gs{j}")
                for ci in range(4):
                    csl = slice(ci * 128, (ci + 1) * 128)
                    pa, pg = self.bank(), self.bank()
                    for k in range(16):
                        ws_ = wbsa if k < 8 else wbsb
                        self.mm(pa[:], OS[:, k, csl], ws_[:, k % 8, :], start=(k == 0), stop=(k == 15))
                    for k in range(8):
                        self.mm(pg[:], self.HT[q][:, k, csl], wgs[:, k, :], start=(k == 0), stop=(k == 7))
                    s_ = sg[ci % 2]
                    self.act(s_[:], pg[:], ACT.Sigmoid)
                    self.tt(m2[:], s_[:], pa[:], ALU.mult)
                    self.tt(merged[:, ci, j * 512:(j + 1) * 512], m2[:], m1[:, ci, :], ALU.add)
            self.aoff = mark + 8192
            S.barrier()
            mT = self.A("mT", [128, 8, 512], BF16)
            for ci in range(4):
                pb = self.bank()
                pbb = pb[:].bitcast(BF16)
                for c in range(8):
                    self.tr(pbb[:, c * 128:(c + 1) * 128], merged[:, ci, c * 128:(c + 1) * 128], ident_b)
                self.cp(mT[:, :, ci * 128:(ci + 1) * 128], pbb.rearrange("p (c t) -> p c t", c=8), eng="act")
            wm = [self.wload(l, "mx0"), self.wload(l, "mx1")]
            for ci in range(4):
                csl = slice(ci * 128, (ci + 1) * 128)
                b2 = [self.bank(), self.bank()]
                for j in range(2):
                    for k in range(8):
                        self.mm(b2[j][:], mT[:, k, csl], wm[j][:, k, :], start=(k == 0), stop=(k == 7))
                postnorm_add(l, 0, 4 * q + ci, b2)
            self.aoff = mark
            S.barrier()

        def mixer(l):
            mark = self.aoff
            for q in range(NQ):
                self.aoff = mark
                S.barrier()
                OS = self.A("OS", [128, 16, 512], BF16)
                OA = self.A("OA", [128, 8, 512], BF16)
                if KSTOP >= 2:
                    ssd_pass(l, q, OS)
                if KSTOP >= 3:
                    attn_pass(l, q, OA)
                if KSTOP >= 4:
                    merge_pass(l, q, OS, OA)
            self.aoff = mark
            S.barrier()

        def xattn(l, b):
            mark = self.aoff
            S.barrier()
            load_gb(l, 1)
            kTm = self.A("kTm", [128, 8, 256], BF16)
            vm = self.A("vm", [128, 2, D], BF16)
            memT = self.A("memT", [128, 8, 256], BF16)
            memt = [self.A("memt", [128, D]) for _ in range(2)]
            for mt in range(2):
                self.dma("sp", memt[mt][:], mem_in[b, mt * 128:(mt + 1) * 128, :], memt[mt], after_bar=True)
            for mt in range(2):
                junk = self.A("junk", [128, D], BF16)
                mn = self.A("mn", [128, D], BF16)
                ss = self.stat()
                self.act(junk[:], memt[mt][:], ACT.Square, accum=ss)
                rs = self.rstd(ss, D)
                self.stt(mn[:], memt[mt][:], rs, GB[:], ALU.mult, ALU.mult)
                pb = self.bank()
                pbb = pb[:].bitcast(BF16)
                for c in range(8):
                    self.tr(pbb[:, c * 128:(c + 1) * 128], mn[:, c * 128:(c + 1) * 128], ident_b)
                self.cp(memT[:, :, mt * 128:(mt + 1) * 128], pbb.rearrange("p (c t) -> p c t", c=8))
            for j in range(2):
                wk = self.wload(l, f"xk{j}")
                for c in range(4):
                    pb = self.bank()
                    for k in range(8):
                        self.mm(pb[:, 0:256], wk[:, k, c * 128:(c + 1) * 128], memT[:, k, :], start=(k == 0), stop=(k == 7))
                    self.cp(kTm[:, j * 4 + c, :], pb[:, 0:256], eng="act")
            for j in range(2):
                wv = self.wload(l, f"xv{j}")
                for mt in range(2):
                    pb = self.bank()
                    for k in range(8):
                        self.mm(pb[:], memT[:, k, mt * 128:(mt + 1) * 128], wv[:, k, :], start=(k == 0), stop=(k == 7))
                    self.cp(vm[:, mt, j * 512:(j + 1) * 512], pb[:])
            load_gb(l, 2)
            self.aoff = mark + 8192 + 4096
            mark1 = self.aoff
            for hf in range(NQ // 2 if NQ >= 2 else 1):
                self.aoff = mark1
                S.barrier()
                qs = list(range(hf * 2, min(hf * 2 + 2, NQ)))
                prenorm(l, 1, [i for jq in qs for i in range(jq * 4, jq * 4 + 4)])
                OX = self.A("OX", [128, 8, 512 * len(qs)], BF16)
                qTh = [self.A("qTh", [128, 2, 512], BF16) for _ in range(2)]
                pT = [self.A("pT", [128, 2, 512], BF16) for _ in range(2)]
                rden = [self.A("rden", [128, 512]) for _ in range(2)]
                n_ = 0
                for hd in range(4):
                    wq = self.wload(l, f"xq{hd}")
                    for jj, jq in enumerate(qs):
                        qt, p_, rd = qTh[n_ % 2], pT[n_ % 2], rden[n_ % 2]
                        n_ += 1
                        for cc in range(2):
                            pb = self.bank()
                            for k in range(8):
                                self.mm(pb[:], wq[:, k, cc * 128:(cc + 1) * 128], self.HT[jq][:, k, :], start=(k == 0), stop=(k == 7))
                            self.act(qt[:, cc, :], pb[:], ACT.Copy, scale=1.0 / 16)
                        for mt in range(2):
                            pb = self.bank()
                            for cc in range(2):
                                self.mm(pb[:], kTm[:, hd * 2 + cc, mt * 128:(mt + 1) * 128], qt[:, cc, :], start=(cc == 0), stop=(cc == 1))
                            self.act(p_[:, mt, :], pb[:], ACT.Exp)
                        pd = self.bank()
                        for mt in range(2):
                            self.mm(pd[:], ones_b, p_[:, mt, :], start=(mt == 0), stop=(mt == 1))
                        self.recip(rd[:], pd[:])
                        for cc in range(2):
                            pb = self.bank()
                            for mt in range(2):
                                self.mm(pb[:], vm[:, mt, hd * 256 + cc * 128:hd * 256 + (cc + 1) * 128], p_[:, mt, :],
                                        start=(mt == 0), stop=(mt == 1))
                            self.tt(OX[:, hd * 2 + cc, jj * 512:(jj + 1) * 512], pb[:], rd[:], ALU.mult)
                wo = [self.wload(l, "xo0"), self.wload(l, "xo1")]
                for ti in range(4 * len(qs)):
                    csl = slice(ti * 128, (ti + 1) * 128)
                    b2 = [self.bank(), self.bank()]
                    for j in range(2):
                        for k in range(8):
                            self.mm(b2[j][:], OX[:, k, csl], wo[j][:, k, :], start=(k == 0), stop=(k == 7))
                    postnorm_add(l, 2, qs[0] * 4 + ti, b2)
            self.aoff = mark
            S.barrier()

        def ffn(l):
            mark = self.aoff
            S.barrier()
            load_gb(l, 3)
            for hf in range(max(NQ // 2, 1)):
                self.aoff = mark
                S.barrier()
                qs = list(range(hf * 2, min(hf * 2 + 2, NQ)))
                prenorm(l, 2, [i for jq in qs for i in range(jq * 4, jq * 4 + 4)])
                actT = self.A("actT", [128, 22, 512 * len(qs)], BF16)
                sgb = [self.A("sgb", [128, 512], BF16) for _ in range(2)]
                n_ = 0
                for fs in range(11):
                    w = self.wload(l, f"gu{fs}")
                    for fcc in range(2):
                        f = fs * 2 + fcc
                        for jj, jq in enumerate(qs):
                            pg, pu = self.bank(), self.bank()
                            for k in range(8):
                                self.mm(pg[:], w[:, k, fcc * 128:(fcc + 1) * 128], self.HT[jq][:, k, :], start=(k == 0), stop=(k == 7))
                            for k in range(8):
                                self.mm(pu[:], w[:, k, 256 + fcc * 128:256 + (fcc + 1) * 128], self.HT[jq][:, k, :], start=(k == 0), stop=(k == 7))
                            s_ = sgb[n_ % 2]
                            n_ += 1
                            self.act(s_[:], pg[:], ACT.Silu)
                            self.tt(actT[:, f, jj * 512:(jj + 1) * 512], s_[:], pu[:], ALU.mult)
                ntile = 4 * len(qs)
                for g0 in range(0, ntile, 4):
                    tiles = list(range(g0, min(g0 + 4, ntile)))
                    bks = {ti: [self.banks[2 * (ti - g0)], self.banks[2 * (ti - g0) + 1]] for ti in tiles}
                    for s in range(6):
                        wd = self.wload(l, f"dn{s}")
                        kcn = 4 if s < 5 else 2
                        for ti in tiles:
                            csl = slice(ti * 128, (ti + 1) * 128)
                            for kk in range(kcn):
                                f = s * 4 + kk
                                for j in range(2):
                                    self.mm(bks[ti][j][:], actT[:, f, csl], wd[:, kk, j * 512:(j + 1) * 512],
                                            start=(f == 0), stop=(f == 21))
                    for ti in tiles:
                        postnorm_add(l, 3, qs[0] * 4 + ti, bks[ti])
            self.aoff = mark
            S.barrier()

        for b in range(nseq):
            for i in range(NT):
                self.dma("sp", self.X[i][:], x_in[b, i * 128:(i + 1) * 128, :], self.X[i])
            for l in self.layers:
                mark = self.aoff
                if KSTOP >= 1:
                    prenorm(l, 0, range(NT))
                if KSTOP >= 2:
                    mixer(l)
                if KSTOP >= 5:
                    xattn(l, b)
                if KSTOP >= 6:
                    ffn(l)
                self.aoff = mark
            for i in range(NT):
                self.dma("sp", y_out[b, i * 128:(i + 1) * 128, :], self.X[i][:], self.X[i], store=True)
        S.emit(self.es)
        self.es.close()
        return nc


def host_inputs(inp, depth):
    wpack = np.stack([pack_layer(inp, l) for l in range(depth)])
    cst = make_consts().reshape(128, 8 * 128)
    gcols = np.zeros((128, depth * 24), np.float32)
    convT = np.zeros((128, depth * 120), np.float32)
    for l in range(depth):
        for w, nm in enumerate(("g_pre_mix", "g_pre_xa", "g_pre_ffn")):
            gcols[:, l * 24 + w * 8:l * 24 + (w + 1) * 8] = np.asarray(inp[nm][l]).reshape(8, 128).T
        cw = np.asarray(inp["conv_w"][l]).reshape(4, 24, 128)
        for k in range(4):
            convT[:, l * 120 + k:l * 120 + 96:4] = cw[k].T
        convT[:, l * 120 + 96:l * 120 + 120] = np.asarray(inp["conv_b"][l]).reshape(24, 128).T
    hparams = np.stack([np.asarray(inp["dt_bias"])[:depth].reshape(-1), np.asarray(inp["a_log"])[:depth].reshape(-1),
                        np.asarray(inp["d_skip"])[:depth].reshape(-1)]).astype(np.float32)
    grows = np.zeros((depth, 5, D), np.float32)
    for l in range(depth):
        grows[l, 0] = inp["g_post_mix"][l]
        grows[l, 1] = inp["g_mem"][l]
        grows[l, 2] = inp["g_post_xa"][l]
        grows[l, 3] = inp["g_post_ffn"][l]
    gssm = np.ascontiguousarray(np.asarray(inp["g_ssm_norm"])[:depth], dtype=np.float32)
    return dict(wpack=wpack, cst=cst, gcols=gcols, convT=convT, hparams=hparams, grows=grows, gssm=gssm)


def run(inp, n_cores, depth, trace=False):
    inp = {k: np.asarray(v, dtype=np.float32) for k, v in inp.items()}
    B, seq, _ = inp["x"].shape
    nseq = B // n_cores
    bld = Builder(nseq, seq, depth)
    nc = bld.build()
    shared = host_inputs(inp, depth)
    in_maps = []
    for c in range(n_cores):
        m = dict(shared)
        m["x"] = np.ascontiguousarray(inp["x"][c * nseq:(c + 1) * nseq])
        m["mem"] = np.ascontiguousarray(inp["mem"][c * nseq:(c + 1) * nseq])
        in_maps.append(m)
    res = run_bass_kernel_spmd(nc, in_maps, core_ids=list(range(n_cores)), trace=trace)
    out = np.concatenate([res.results[c]["y"] for c in range(n_cores)], axis=0)
    return out, res


def kernel(**inputs):
    out, _ = run(inputs, 8, 4)
    return out.astype(np.float32)
```
